# Optimizing a Trainium2 kernel written in Bass

```python
import jax
import jax.numpy as jnp
from jax import lax
import numpy as np

D_MODEL = 2048
BATCH = 4
SEQ = 2048
DEPTH = 4
DEC_BATCH = 8
DEC_SEQ = 1
PAST_LEN = 16384
PAGE_SIZE = 128

HEAD_DIM = 64
NORM_EPS = 1e-6

RWKV_HEADS = 12
RWKV_WIDTH = RWKV_HEADS * HEAD_DIM
DECAY_LORA = 64
ICLR_LORA = 64
RWKV_COLS = 4 * RWKV_WIDTH + DECAY_LORA + ICLR_LORA
GN_EPS = 64e-5

ATT_GROUPS = ((128, 1), (512, 4), (2048, 16))
HEADS_PER_GROUP = 4
ATT_HEADS = len(ATT_GROUPS) * HEADS_PER_GROUP
ATT_WIDTH = ATT_HEADS * HEAD_DIM
ATT_COLS = 4 * ATT_WIDTH
ROPE_THETA = 500000.0
ROPE_DIMS = HEAD_DIM // 4
BLOCK = 128

POOL_WINDOWS = (2, 4, 8, 16)
POOL_GROUP = 128
POOL_WIDTH = len(POOL_WINDOWS) * POOL_GROUP
POOL_COLS = 2 * POOL_WIDTH
POOL_BUF = max(POOL_WINDOWS) - 1

D_MIX = RWKV_WIDTH + ATT_WIDTH + POOL_WIDTH
D_IN = RWKV_COLS + ATT_COLS + POOL_COLS

kernel_name = 'hybrid_rwkv7_dilated_swa_pool_step'


def rms_norm(x, w, eps=NORM_EPS):
    xf = x.astype(jnp.float32)
    y = xf * lax.rsqrt(jnp.mean(xf * xf, axis=-1, keepdims=True) + eps)
    return (y * w.astype(jnp.float32)).astype(x.dtype)


def rope(x, pos):
    half = ROPE_DIMS // 2
    inv = jnp.power(jnp.float32(ROPE_THETA), -jnp.arange(half, dtype=jnp.float32) * 2.0 / ROPE_DIMS)
    ang = pos[:, None] * inv[None, :]
    cos = jnp.cos(ang)[None, :, None, :]
    sin = jnp.sin(ang)[None, :, None, :]
    x1 = x[..., :half]
    x2 = x[..., half:ROPE_DIMS]
    return jnp.concatenate([x1 * cos - x2 * sin, x2 * cos + x1 * sin, x[..., ROPE_DIMS:]], axis=-1)


def rwkv_time_mix(z, z_prev0, s0, mu, w0, w_up, a0, a_up, k_k, k_a, r_k, ln_w, ln_b):
    B, T, _ = z.shape
    W = RWKV_WIDTH
    z_prev = jnp.concatenate([z_prev0[:, None], z[:, :-1]], axis=1)
    zs = z + (z_prev - z) * mu
    r, k, v, g = (zs[..., i * W:(i + 1) * W] for i in range(4))
    w_dn = zs[..., 4 * W:4 * W + DECAY_LORA]
    a_dn = zs[..., 4 * W + DECAY_LORA:]
    w_log = -jax.nn.softplus(-(w0 + jnp.tanh(w_dn) @ w_up)) - 0.5
    decay = jnp.exp(-jnp.exp(w_log))
    a = jax.nn.sigmoid(a0 + a_dn @ a_up)
    heads = lambda t: t.reshape(B, T, RWKV_HEADS, HEAD_DIM)
    kk = heads(k * k_k)
    kk = kk / jnp.maximum(jnp.sqrt(jnp.sum(kk * kk, axis=-1, keepdims=True)), 1e-12)
    k = k * (1.0 + (a - 1.0) * k_a)
    r, k, v, decay, a = heads(r), heads(k), heads(v), heads(decay), heads(a)

    def step(S, inp):
        r_t, w_t, k_t, v_t, kk_t, a_t = inp
        sa = jnp.einsum('bhvk,bhk->bhv', S, -kk_t)
        S = (S * w_t[:, :, None, :] + sa[..., None] * (kk_t * a_t)[:, :, None, :]
             + v_t[..., None] * k_t[:, :, None, :])
        return S, jnp.einsum('bhvk,bhk->bhv', S, r_t)

    xs = tuple(jnp.moveaxis(t, 1, 0) for t in (r, decay, k, v, kk, a))
    s_last, o = lax.scan(step, s0, xs)
    o = jnp.moveaxis(o, 0, 1)
    mean = jnp.mean(o, axis=-1, keepdims=True)
    var = jnp.mean(jnp.square(o - mean), axis=-1, keepdims=True)
    o = ((o - mean) * lax.rsqrt(var + GN_EPS)).reshape(B, T, W) * ln_w + ln_b
    o = o + (jnp.sum(r * k * r_k, axis=-1, keepdims=True) * v).reshape(B, T, W)
    return o * jax.nn.silu(g), s_last


def dilated_band_attention(q, k, v, window, dil):
    B, S, H, E = q.shape
    span = window // dil
    n_sub = -(-S // dil)
    n_blk = -(-n_sub // BLOCK)
    pad = n_blk * BLOCK * dil - S

    def to_blocks(t):
        t = jnp.pad(t, ((0, 0), (0, pad), (0, 0), (0, 0)))
        return t.reshape(B, n_blk, BLOCK, dil, H, E)

    def with_prev(t):
        prev = jnp.pad(t[:, :-1], ((0, 0), (1, 0), (0, 0), (0, 0), (0, 0), (0, 0)))
        return jnp.concatenate([prev, t], axis=2)

    qb = to_blocks(q)
    kc = with_prev(to_blocks(k))
    vc = with_prev(to_blocks(v))
    s = jnp.einsum('bnidhe,bnjdhe->bndhij', qb, kc) * (E ** -0.5)
    qi = jnp.arange(BLOCK)[:, None]
    kj = jnp.arange(2 * BLOCK)[None, :]
    dist = BLOCK + qi - kj
    band = (dist >= 0) & (dist <= span)
    has_prev = (jnp.arange(n_blk) > 0)[:, None, None] | (kj >= BLOCK)[None]
    mask = band[None] & has_prev
    s = jnp.where(mask[None, :, None, None], s, -jnp.inf)
    m = jnp.max(s, axis=-1, keepdims=True)
    p = jnp.exp(s - m)
    l = jnp.sum(p, axis=-1, keepdims=True)
    o = jnp.einsum('bndhij,bnjdhe->bndhie', p / l, vc)
    lse = (m + jnp.log(l))[..., 0]
    o = jnp.transpose(o, (0, 1, 4, 2, 3, 5)).reshape(B, n_blk * BLOCK * dil, H, E)[:, :S]
    lse = jnp.transpose(lse, (0, 1, 4, 2, 3)).reshape(B, n_blk * BLOCK * dil, H)[:, :S]
    return o, lse


def dilated_gather_attention(q, k_all, v_all, n_buf, window, dil):
    T, E = q.shape[1], q.shape[-1]
    span = window // dil
    idx = n_buf + jnp.arange(T)[:, None] - dil * jnp.arange(span + 1)[None, :]
    valid = idx >= 0
    idx = jnp.maximum(idx, 0)
    kg = k_all[:, idx]
    vg = v_all[:, idx]
    s = jnp.einsum('bthe,btjhe->bthj', q, kg) * (E ** -0.5)
    s = jnp.where(valid[None, :, None, :], s, -jnp.inf)
    m = jnp.max(s, axis=-1, keepdims=True)
    p = jnp.exp(s - m)
    l = jnp.sum(p, axis=-1, keepdims=True)
    o = jnp.einsum('bthj,btjhe->bthe', p / l, vg)
    return o, (m + jnp.log(l))[..., 0]


def attention_branch(za, pos, kv_bufs, qn_w, kn_w):
    B, T, _ = za.shape
    q, k, v, g = (za[..., i * ATT_WIDTH:(i + 1) * ATT_WIDTH] for i in range(4))
    heads = lambda t: t.reshape(B, T, ATT_HEADS, HEAD_DIM)
    q = rope(rms_norm(heads(q), qn_w), pos)
    k = rope(rms_norm(heads(k), kn_w), pos)
    v = heads(v)
    outs, lses, new_kv = [], [], []
    for gi, (window, dil) in enumerate(ATT_GROUPS):
        hs = slice(gi * HEADS_PER_GROUP, (gi + 1) * HEADS_PER_GROUP)
        qg, kg, vg = q[:, :, hs], k[:, :, hs], v[:, :, hs]
        if kv_bufs is None:
            o, lse = dilated_band_attention(qg, kg, vg, window, dil)
            keep = min(window, T)
            new_kv.append((kg[:, T - keep:], vg[:, T - keep:]))
        else:
            k_buf, v_buf = kv_bufs[gi]
            k_all = jnp.concatenate([k_buf.astype(jnp.float32), kg], axis=1)
            v_all = jnp.concatenate([v_buf.astype(jnp.float32), vg], axis=1)
            o, lse = dilated_gather_attention(qg, k_all, v_all, k_buf.shape[1], window, dil)
            new_kv.append((kg, vg))
        outs.append(o)
        lses.append(lse)
    alpha = jax.nn.softmax(jnp.stack(lses, axis=2), axis=2)
    o = (jnp.stack(outs, axis=2) * alpha[..., None]).reshape(B, T, ATT_WIDTH)
    return o * jax.nn.silu(g), new_kv


def pool_branch(zp, pool_prev, pos, pool_w, pool_scale):
    B, T, _ = zp.shape
    u, g = zp[..., :POOL_WIDTH], zp[..., POOL_WIDTH:]
    u_ext = u if pool_prev is None else jnp.concatenate([pool_prev.astype(jnp.float32), u], axis=1)
    n_prev = u_ext.shape[1] - T
    cs = jnp.pad(jnp.cumsum(u_ext, axis=1), ((0, 0), (1, 0), (0, 0)))
    row = n_prev + jnp.arange(T) + 1
    hi = cs[:, row]
    means = []
    for gi, w in enumerate(POOL_WINDOWS):
        ch = slice(gi * POOL_GROUP, (gi + 1) * POOL_GROUP)
        lo = cs[:, jnp.maximum(row - w, 0), ch]
        cnt = jnp.minimum(jnp.float32(w), pos + 1.0)
        means.append((hi[..., ch] - lo) / cnt[None, :, None])
    d = jnp.concatenate(means, axis=-1) - u
    d = jnp.einsum('btgc,gcd->btgd', d.reshape(B, T, len(POOL_WINDOWS), POOL_GROUP), pool_w)
    d = d.reshape(B, T, POOL_WIDTH) * pool_scale
    return d * jax.nn.silu(g), u_ext[:, -POOL_BUF:]


def trunk_layer(x, pos, shift0, wkv0, pool_prev, kv_bufs, norm_w, w_in, w_out, mu, w0, w_up, a0, a_up,
                k_k, k_a, r_k, ln_w, ln_b, qn_w, kn_w, pool_w, pool_scale):
    h = rms_norm(x, norm_w)
    z = jnp.einsum('btd,dc->btc', h, w_in).astype(jnp.float32)
    z_a = z[..., :RWKV_COLS]
    z_b = z[..., RWKV_COLS:RWKV_COLS + ATT_COLS]
    z_c = z[..., RWKV_COLS + ATT_COLS:]
    y_a, wkv_new = rwkv_time_mix(z_a, shift0, wkv0, mu, w0, w_up, a0, a_up, k_k, k_a, r_k, ln_w, ln_b)
    y_b, kv_new = attention_branch(z_b, pos, kv_bufs, qn_w, kn_w)
    y_c, pool_new = pool_branch(z_c, pool_prev, pos, pool_w, pool_scale)
    mix = jnp.concatenate([y_a, y_b, y_c], axis=-1).astype(x.dtype)
    y = x + jnp.einsum('btc,cd->btd', mix, w_out)
    return y, z_a[:, -1], wkv_new, pool_new, kv_new


def stack_states(per_layer, dt):
    st = lambda f: jnp.stack([f(s) for s in per_layer]).astype(dt)
    wkv = st(lambda s: s[1])
    shift = st(lambda s: s[0])
    pool = st(lambda s: s[2])
    kv = [st(lambda s, g=g, j=j: s[3][g][j]) for g in range(len(ATT_GROUPS)) for j in range(2)]
    return (wkv, shift, pool, *kv)


def setup_inputs(seed: int = 0) -> dict:
    key = jax.random.key(seed)
    ks = jax.random.split(key, 28)
    f32 = jnp.float32
    nrm = lambda k, shape, s: jax.random.normal(k, shape, f32) * s
    near_one = lambda k, shape: 1.0 + nrm(k, shape, 0.02)
    lens = [min(w, PAST_LEN) for w, _ in ATT_GROUPS]
    kv_shape = lambda n: (DEPTH, DEC_BATCH, n, HEADS_PER_GROUP, HEAD_DIM)
    return {
        'x_prompt': nrm(ks[0], (BATCH, SEQ, D_MODEL), 1.0),
        'x_sample': nrm(ks[1], (DEC_BATCH, DEC_SEQ, D_MODEL), 1.0),
        'state_wkv': nrm(ks[2], (DEPTH, DEC_BATCH, RWKV_HEADS, HEAD_DIM, HEAD_DIM), 1.0),
        'state_shift': nrm(ks[3], (DEPTH, DEC_BATCH, RWKV_COLS), 1.0),
        'state_pool': nrm(ks[4], (DEPTH, DEC_BATCH, POOL_BUF, POOL_WIDTH), 1.0),
        'cache_k_w128': nrm(ks[5], kv_shape(lens[0]), 1.0),
        'cache_v_w128': nrm(ks[6], kv_shape(lens[0]), 1.0),
        'cache_k_w512': nrm(ks[7], kv_shape(lens[1]), 1.0),
        'cache_v_w512': nrm(ks[8], kv_shape(lens[1]), 1.0),
        'cache_k_w2048': nrm(ks[9], kv_shape(lens[2]), 1.0),
        'cache_v_w2048': nrm(ks[10], kv_shape(lens[2]), 1.0),
        'norm_w': near_one(ks[11], (DEPTH, D_MODEL)),
        'w_in': nrm(ks[12], (DEPTH, D_MODEL, D_IN), D_MODEL ** -0.5),
        'w_out': nrm(ks[13], (DEPTH, D_MIX, D_MODEL), D_MIX ** -0.5),
        'rwkv_mu': jax.random.uniform(ks[14], (DEPTH, RWKV_COLS), f32),
        'rwkv_w0': jax.random.uniform(ks[15], (DEPTH, RWKV_WIDTH), f32, -6.0, -1.0),
        'rwkv_w_up': nrm(ks[16], (DEPTH, DECAY_LORA, RWKV_WIDTH), 0.5 * DECAY_LORA ** -0.5),
        'rwkv_a0': nrm(ks[17], (DEPTH, RWKV_WIDTH), 0.1),
        'rwkv_a_up': nrm(ks[18], (DEPTH, ICLR_LORA, RWKV_WIDTH), 0.5 * ICLR_LORA ** -0.5),
        'rwkv_k_k': 0.85 + nrm(ks[19], (DEPTH, RWKV_WIDTH), 0.02),
        'rwkv_k_a': near_one(ks[20], (DEPTH, RWKV_WIDTH)),
        'rwkv_r_k': nrm(ks[21], (DEPTH, RWKV_HEADS, HEAD_DIM), 0.1),
        'rwkv_ln_w': near_one(ks[22], (DEPTH, RWKV_WIDTH)),
        'rwkv_ln_b': nrm(ks[23], (DEPTH, RWKV_WIDTH), 0.02),
        'q_norm_w': near_one(ks[24], (DEPTH, HEAD_DIM)),
        'k_norm_w': near_one(ks[25], (DEPTH, HEAD_DIM)),
        'pool_w': nrm(ks[26], (DEPTH, len(POOL_WINDOWS), POOL_GROUP, POOL_GROUP), POOL_GROUP ** -0.5),
        'pool_scale': near_one(ks[27], (DEPTH, POOL_WIDTH)),
    }


def reference(x_prompt, x_sample, state_wkv, state_shift, state_pool,
              cache_k_w128, cache_v_w128, cache_k_w512, cache_v_w512, cache_k_w2048, cache_v_w2048,
              norm_w, w_in, w_out, rwkv_mu, rwkv_w0, rwkv_w_up, rwkv_a0, rwkv_a_up,
              rwkv_k_k, rwkv_k_a, rwkv_r_k, rwkv_ln_w, rwkv_ln_b, q_norm_w, k_norm_w, pool_w, pool_scale):
    dt = x_prompt.dtype
    B, S, _ = x_prompt.shape
    T = x_sample.shape[1]
    pos_p = jnp.arange(S, dtype=jnp.float32)
    pos_s = PAST_LEN + jnp.arange(T, dtype=jnp.float32)
    cache_k = (cache_k_w128, cache_k_w512, cache_k_w2048)
    cache_v = (cache_v_w128, cache_v_w512, cache_v_w2048)
    zero_shift = jnp.zeros((B, RWKV_COLS), jnp.float32)
    zero_wkv = jnp.zeros((B, RWKV_HEADS, HEAD_DIM, HEAD_DIM), jnp.float32)
    hp, hs = x_prompt, x_sample
    p_layers, s_layers = [], []
    for l in range(DEPTH):
        lw = (norm_w[l], w_in[l], w_out[l], rwkv_mu[l], rwkv_w0[l], rwkv_w_up[l], rwkv_a0[l], rwkv_a_up[l],
              rwkv_k_k[l], rwkv_k_a[l], rwkv_r_k[l], rwkv_ln_w[l], rwkv_ln_b[l], q_norm_w[l], k_norm_w[l],
              pool_w[l], pool_scale[l])
        hp, p_shift_l, p_wkv_l, p_pool_l, p_kv_l = trunk_layer(hp, pos_p, zero_shift, zero_wkv, None, None, *lw)
        bufs = [(cache_k[g][l], cache_v[g][l]) for g in range(len(ATT_GROUPS))]
        hs, s_shift_l, s_wkv_l, s_pool_l, s_kv_l = trunk_layer(
            hs, pos_s, state_shift[l].astype(jnp.float32), state_wkv[l].astype(jnp.float32),
            state_pool[l], bufs, *lw)
        p_layers.append((p_shift_l, p_wkv_l, p_pool_l, p_kv_l))
        s_layers.append((s_shift_l, s_wkv_l, s_pool_l, s_kv_l))
    p_wkv, p_shift, p_pool, p_k128, p_v128, p_k512, p_v512, p_k2048, p_v2048 = stack_states(p_layers, dt)
    s_wkv, s_shift, s_pool, s_k128, s_v128, s_k512, s_v512, s_k2048, s_v2048 = stack_states(s_layers, dt)
    return (hp, hs,
            p_wkv, p_shift, p_pool, p_k128, p_v128, p_k512, p_v512, p_k2048, p_v2048,
            s_wkv, s_shift, s_pool, s_k128, s_v128, s_k512, s_v512, s_k2048, s_v2048)
```

```python
import contextlib
import math
import numpy as np
import concourse.bass as bass
import concourse.mybir as mybir
from concourse.bass_utils import run_bass_kernel_spmd

F32 = mybir.dt.float32
BF16 = mybir.dt.bfloat16
I32 = mybir.dt.int32
AF = mybir.ActivationFunctionType
ALU = mybir.AluOpType
AX = mybir.AxisListType

D = 2048
S = 2048
NT = 2176
NCH = 17
DIN = 7296
NORM_EPS = 1e-6
GN_EPS = 64e-5
C0 = math.exp(-0.5)
ROPE_THETA = 500000.0
PAST = 16384.0
PREP_W = 4
LORA_DELAY = 3
FUSE_NORM = True
SIM_SW_SEMS = False


class Buf:
    __slots__ = ("t", "lw", "rd", "name")

    def __init__(self, t=None, name=""):
        self.t = t
        self.lw = None
        self.rd = {}
        self.name = name

    def __getitem__(self, k):
        return self.t[k]


class Trk:
    def __init__(self, nc, n_dma_sems=32):
        self.nc = nc
        self.es = contextlib.ExitStack()
        self.eng = {"pe": nc.tensor, "act": nc.scalar, "dve": nc.vector, "pool": nc.gpsimd, "sp": nc.sync}
        self.sem = {}
        self.cnt = {}
        for e in self.eng:
            self.sem[e] = self.es.enter_context(nc.semaphore("sem_" + e))
            self.cnt[e] = 0
        self.dsem = [self.es.enter_context(nc.semaphore("dsem%d" % i)) for i in range(n_dma_sems)]
        self.dcnt = [0] * n_dma_sems
        self.dnext = 0
        self.waited = {e: {} for e in self.eng}
        self.nbuf = 0
        self.ninst = 0
        self.scopes = []
        self.psem = {}

    def push(self):
        st = contextlib.ExitStack()
        self.scopes.append(st)

    def pop(self):
        self.barrier()
        self.scopes.pop().close()

    def _stack(self):
        return self.scopes[-1] if self.scopes else self.es

    def sb(self, shape, dt, name=None):
        self.nbuf += 1
        name = (name or "sb") + "_%d" % self.nbuf
        t = self._stack().enter_context(self.nc.sbuf_tensor(name, list(shape), dt))
        return Buf(t, name)

    def ps(self, shape, dt, name=None):
        self.nbuf += 1
        name = (name or "ps") + "_%d" % self.nbuf
        t = self.es.enter_context(self.nc.psum_tensor(name, list(shape), dt))
        return Buf(t, name)

    def tok(self, name=""):
        return Buf(None, name)

    def _need(self, e, deps):
        eng = self.eng[e]
        w = self.waited[e]
        best = {}
        for (s, v) in deps:
            k = id(s)
            if v > w.get(k, 0) and v > best.get(k, (None, 0))[1]:
                best[k] = (s, v)
        for k, (s, v) in best.items():
            eng.wait_ge(s, v)
            w[k] = v
            self.ninst += 1

    def _deps(self, e, r, w):
        deps = []
        for b in r:
            if b.lw is not None:
                deps.append(b.lw)
        for b in w:
            if b.lw is not None:
                deps.append(b.lw)
            deps.extend(b.rd.values())
        if e == "pe":
            deps = [d for d in deps if d[0] is not self.sem["pe"]]
        return deps

    def _mark(self, sig, r, w):
        k = id(sig[0])
        for b in r:
            if b.rd.get(k, (None, 0))[1] < sig[1]:
                b.rd[k] = sig
        for b in w:
            b.lw = sig
            b.rd = {}

    def op(self, e, fn, r=(), w=()):
        self._need(e, self._deps(e, r, w))
        ins = fn(self.eng[e])
        self.cnt[e] += 1
        ins.then_inc(self.sem[e], 1)
        self._mark((self.sem[e], self.cnt[e]), r, w)
        self.ninst += 1
        return ins

    def dma(self, q, out, in_, r=(), w=(), **kw):
        if q == "pool" and SIM_SW_SEMS:
            return self.dma_sw(out, in_, r, w, **kw)
        i = self.dnext
        self.dnext = (self.dnext + 1) % len(self.dsem)
        s = self.dsem[i]
        deps = self._deps(q, r, w)
        if self.dcnt[i] > 0:
            deps.append((s, 16 * self.dcnt[i]))
        self._need(q, deps)
        ins = self.eng[q].dma_start(out=out, in_=in_, **kw)
        self.dcnt[i] += 1
        ins.then_inc(s, 16)
        self._mark((s, 16 * self.dcnt[i]), r, w)
        self.ninst += 1
        return ins

    def dma_sw(self, out, in_, r=(), w=(), **kw):
        b = w[0]
        key = id(b)
        if key not in self.psem:
            self.psem[key] = self.es.enter_context(self.nc.semaphore("psem%d" % len(self.psem)))
        s = self.psem[key]
        self._need("pool", self._deps("pool", r, w))
        self.eng["pool"].sem_clear(s)
        for e in self.waited:
            self.waited[e].pop(id(s), None)
        ins = self.eng["pool"].dma_start(out=out, in_=in_, **kw)
        ins.then_inc(s, 16)
        self._mark((s, 16), r, w)
        self.ninst += 2
        return ins

    def barrier(self):
        sigs = [(self.sem[e], self.cnt[e]) for e in self.eng if self.cnt[e] > 0]
        sigs += [(self.dsem[i], 16 * self.dcnt[i]) for i in range(len(self.dsem)) if self.dcnt[i] > 0]
        for e in self.eng:
            self._need(e, [s for s in sigs if s[0] is not self.sem[e]])

    def finish(self):
        self.barrier()
        while self.scopes:
            self.scopes.pop().close()
        self.es.close()


IN_SPECS = [
    ("x_p", [S, D]), ("x_s", [1, D]), ("st_wkv", [4, 12, 64, 64]), ("st_shift", [4, 3200]),
    ("st_pool", [4, 15, 512]),
    ("ck0", [4, 128, 256]), ("cv0", [4, 128, 256]), ("ck1", [4, 512, 256]), ("cv1", [4, 512, 256]),
    ("ck2", [4, 2048, 256]), ("cv2", [4, 2048, 256]),
    ("norm_w", [4, D]), ("w_in", [4, D, DIN]), ("w_out", [4, D, D]), ("rwkv_mu", [4, 3200]),
    ("rwkv_w0", [4, 768]), ("rwkv_w_up", [4, 64, 768]), ("rwkv_a0", [4, 768]), ("rwkv_a_up", [4, 64, 768]),
    ("rwkv_k_k", [4, 768]), ("rwkv_k_a", [4, 768]), ("rwkv_r_k", [4, 768]), ("rwkv_ln_w", [4, 768]),
    ("rwkv_ln_b", [4, 768]), ("q_norm_w", [4, 64]), ("k_norm_w", [4, 64]), ("pool_w", [4, 4, 128, 128]),
    ("pool_scale", [4, 512]),
]
OUT_SPECS = [
    ("y_p", [S, D]), ("y_s", [1, D]),
    ("p_wkv", [4, 12, 64, 64]), ("p_shift", [4, 3200]), ("p_pool", [4, 15, 512]),
    ("p_k0", [4, 128, 256]), ("p_v0", [4, 128, 256]), ("p_k1", [4, 512, 256]), ("p_v1", [4, 512, 256]),
    ("p_k2", [4, 2048, 256]), ("p_v2", [4, 2048, 256]),
    ("s_wkv", [4, 12, 64, 64]), ("s_shift", [4, 3200]), ("s_pool", [4, 15, 512]),
    ("s_k0", [4, 1, 256]), ("s_v0", [4, 1, 256]), ("s_k1", [4, 1, 256]), ("s_v1", [4, 1, 256]),
    ("s_k2", [4, 1, 256]), ("s_v2", [4, 1, 256]),
]
DILS = (1, 4, 16)


def blk_cols(g, bi):
    if bi == 16:
        return 2048, 1
    if g == 0:
        return bi * 128, 1
    if g == 1:
        return (bi // 4) * 512 + (bi % 4), 4
    return bi, 16


def blk_prev(g, bi):
    if bi == 16:
        return "cache"
    if g == 0:
        return bi - 1 if bi >= 1 else None
    if g == 1:
        return bi - 4 if bi >= 4 else None
    return None


def build(DEPTH=4, phases=("p0", "p1", "p2", "p3", "p4", "p5"), lim=None):
    PCH = list(range(16)) if lim is None else list(lim["chunks"])
    CHS = PCH + [16]
    FMT = list(range(33)) if lim is None else list(lim["tiles"])
    TMK = ("vg", "av", "ag", "pu") if lim is None else tuple(lim["kinds"])
    TBS = sorted(set(c // 4 for c in PCH)) + [4]
    STG = 99 if lim is None else lim.get("stage", 99)
    SUB = "" if lim is None else lim.get("sub", "")
    nc = bass.Bass("TRN2", target_bir_lowering=False)
    I = {n: nc.dram_tensor(n, s, F32, kind="ExternalInput").ap() for n, s in IN_SPECS}
    O = {n: nc.dram_tensor(n, s, F32, kind="ExternalOutput").ap() for n, s in OUT_SPECS}
    scr = lambda n, s: nc.dram_tensor(n, s, F32, kind="Internal").ap()
    zf = scr("zf", [33 * 128, NT])
    zt_vg = scr("zt_vg", [NT, 1536])
    zt_av = scr("zt_av", [3, NT, 256])
    zt_ag = scr("zt_ag", [NT, 768])
    oacc = scr("oacc", [3, NT, 260])
    xbuf = [scr("xbuf0", [S, D]), scr("xbuf1", [S, D])]
    xsb = [scr("xsb0", [1, D]), scr("xsb1", [1, D])]

    T = Trk(nc)
    cm = nc.allow_non_contiguous_dma(reason="small strided parameter loads")
    cm.__enter__()
    cm2 = nc.allow_low_precision(reason="bf16 matmul operands, fp32 accumulate")
    cm2.__enter__()
    NC = True

    def tt(e, out, a, b, op, r, w):
        return T.op(e, lambda E: E.tensor_tensor(out=out, in0=a, in1=b, op=op), r=r, w=w)

    def ts(e, out, a, s1, op0, r, w, s2=None, op1=None):
        if op1 is None:
            return T.op(e, lambda E: E.tensor_scalar(out=out, in0=a, scalar1=s1, scalar2=None, op0=op0), r=r, w=w)
        return T.op(e, lambda E: E.tensor_scalar(out=out, in0=a, scalar1=s1, scalar2=s2, op0=op0, op1=op1), r=r, w=w)

    def stt(e, out, a, s, b, op0, op1, r, w):
        return T.op(e, lambda E: E.scalar_tensor_tensor(out=out, in0=a, scalar=s, in1=b, op0=op0, op1=op1), r=r, w=w)

    def act(out, in_, func, r, w, scale=1.0, bias=None, accum=None):
        kw = {}
        if bias is not None:
            kw["bias"] = bias
        if accum is not None:
            kw["accum_out"] = accum
        return T.op("act", lambda E: E.activation(out=out, in_=in_, func=func, scale=scale, **kw), r=r, w=w)

    def cp(e, out, in_, r, w):
        if e == "act":
            return T.op("act", lambda E: E.copy(out=out, in_=in_), r=r, w=w)
        return T.op(e, lambda E: E.tensor_copy(out=out, in_=in_), r=r, w=w)

    def mm(out, lhsT, rhs, st, sp, r, w):
        return T.op("pe", lambda E: E.matmul(out, lhsT=lhsT, rhs=rhs, start=st, stop=sp), r=r, w=w)

    def tp(out, in_, ident, r, w):
        return T.op("pe", lambda E: E.transpose(out=out, in_=in_, identity=ident), r=r, w=w)

    def ms(e, ap, val, w):
        return T.op(e, lambda E: E.memset(ap, val), w=w)

    ev_i = [0]

    def evac(out, in_, r, w):
        ev_i[0] += 1
        return cp("act" if ev_i[0] % 2 else "dve", out, in_, r, w)

    def bc(ap, shape, axis):
        return ap.unsqueeze(axis).to_broadcast(list(shape))

    pb = [T.ps([128, 512], F32, "bank") for _ in range(8)]
    pbi = [0]

    def nb():
        pbi[0] = (pbi[0] + 1) % 8
        return pb[pbi[0]]

    big = T.sb([128, 16, NT], BF16, "big")
    btok = [T.tok("big%d" % c) for c in range(NCH)]
    ones = T.sb([128, 128], F32, "ones")
    identF = T.sb([128, 128], F32, "identF")
    identB = T.sb([128, 128], BF16, "identB")
    m_su = T.sb([128, 4, 128], BF16, "m_su")
    m_sl = T.sb([128, 4, 128], BF16, "m_sl")
    m_u = T.sb([128, 4, 128], BF16, "m_u")
    amask = T.sb([128, 2, 2, 128], BF16, "amask")
    bones = T.sb([128, 128], F32, "bones")
    bsel = T.sb([128, 2], BF16, "bsel")
    prot = T.sb([128, 128], F32, "prot")
    fixw = T.sb([128, 4, 16], F32, "fixw")
    ropetab = scr("ropetab", [2, 128, NT])

    T.push()
    cosT = T.sb([128, NT], F32, "cosT")
    sinT = T.sb([128, NT], F32, "sinT")
    ms("pool", ones[:], 1.0, [ones])
    asel = lambda out, pat, op, base, cmul, w: T.op(
        "pool", lambda E: E.affine_select(out=out, in_=ones_src(out), pattern=pat, compare_op=op, fill=0.0,
                                          base=base, channel_multiplier=cmul), r=[ones], w=w)

    def ones_src(out):
        shp = out.shape
        if len(shp) == 2:
            return ones[:, 0:shp[1]]
        return bc(ones[:, 0:shp[-1]], shp, 1)

    T.op("pool", lambda E: E.affine_select(out=identF[:], in_=ones[:], pattern=[[-1, 128]], compare_op=ALU.is_equal,
                                           fill=0.0, base=0, channel_multiplier=1), r=[ones], w=[identF])
    cp("dve", identB[:], identF[:], [identF], [identB])
    ones4 = T.sb([128, 4, 128], F32, "ones4")
    ms("pool", ones4[:], 1.0, [ones4])
    T.op("pool", lambda E: E.affine_select(out=m_su[:], in_=ones4[:], pattern=[[0, 4], [1, 128]], compare_op=ALU.is_ge,
                                           fill=0.0, base=-1, channel_multiplier=-1), r=[ones4], w=[m_su])
    T.op("pool", lambda E: E.affine_select(out=m_sl[:], in_=ones4[:], pattern=[[0, 4], [-1, 128]], compare_op=ALU.is_ge,
                                           fill=0.0, base=-1, channel_multiplier=1), r=[ones4], w=[m_sl])
    T.op("pool", lambda E: E.affine_select(out=m_u[:], in_=ones4[:], pattern=[[0, 4], [1, 128]], compare_op=ALU.is_ge,
                                           fill=0.0, base=0, channel_multiplier=-1), r=[ones4], w=[m_u])
    amf = T.sb([128, 2, 2, 128], F32, "amf")
    for h2 in range(2):
        T.op("pool", lambda E, h2=h2: E.affine_select(out=amf[:, h2, 0, :], in_=ones[:], pattern=[[-1, 128]],
                                                      compare_op=ALU.is_ge, fill=0.0, base=0, channel_multiplier=1),
             r=[ones], w=[amf])
        T.op("pool", lambda E, h2=h2: E.affine_select(out=amf[:, h2, 1, :], in_=ones[:], pattern=[[1, 128]],
                                                      compare_op=ALU.is_ge, fill=0.0, base=0, channel_multiplier=-1),
             r=[ones], w=[amf])
    cp("dve", amask[:], amf[:], [amf], [amask])
    ms("pool", bones[:], 0.0, [bones])
    ms("pool", bones[0:64, 0:64], 1.0, [bones])
    ms("pool", bones[64:128, 64:128], 1.0, [bones])
    ms("pool", bsel[:], 0.0, [bsel])
    ms("pool", bsel[0:64, 0:1], 1.0, [bsel])
    ms("pool", bsel[64:128, 1:2], 1.0, [bsel])
    t1 = T.sb([128, 128], F32, "t1")
    t2 = T.sb([128, 128], F32, "t2")
    T.op("pool", lambda E: E.affine_select(out=t1[:], in_=ones[:], pattern=[[-1, 128]], compare_op=ALU.is_equal,
                                           fill=0.0, base=-8, channel_multiplier=1), r=[ones], w=[t1])
    T.op("pool", lambda E: E.affine_select(out=t2[:], in_=ones[:], pattern=[[-1, 128]], compare_op=ALU.is_equal,
                                           fill=0.0, base=8, channel_multiplier=1), r=[ones], w=[t2])
    for base in (0, 64):
        ms("pool", t1[:, base + 8:base + 64], 0.0, [t1])
        ms("pool", t2[:, base:base + 8], 0.0, [t2])
        ms("pool", t2[:, base + 16:base + 64], 0.0, [t2])
    tt("pool", prot[:], t2[:], t1[:], ALU.subtract, [t1, t2], [prot])
    ms("pool", fixw[:], 1.0, [fixw])
    for gi in range(4):
        w_ = 2 ** (gi + 1)
        for t_ in range(w_ - 1):
            ms("pool", fixw[:, gi, t_:t_ + 1], float(w_) / float(t_ + 1), [fixw])
    invrow = T.sb([1, 128], F32, "invrow")
    posrow = T.sb([1, NT], F32, "posrow")
    posi = T.sb([1, NT], I32, "posi")
    ms("pool", invrow[:], 0.0, [invrow])
    for base in (0, 64):
        for e_ in range(16):
            val = float(np.float32(ROPE_THETA) ** np.float32(-(e_ % 8) * 2.0 / 16.0))
            ms("pool", invrow[0:1, base + e_:base + e_ + 1], val, [invrow])
    T.op("pool", lambda E: E.iota(posi[:], pattern=[[1, NT]], base=0, channel_multiplier=0), w=[posi])
    cp("dve", posrow[:], posi[:], [posi], [posrow])
    ms("dve", posrow[0:1, 2048:NT], PAST, [posrow])
    yy = T.sb([128, 512], F32, "yy")
    yi = T.sb([128, 512], I32, "yi")
    yf = T.sb([128, 512], F32, "yf")
    for tb in range(5):
        n = 512 if tb < 4 else 128
        cs_ = slice(tb * 512, tb * 512 + n)
        bk = nb()
        mm(bk[:, 0:n], invrow[0:1, :], posrow[0:1, cs_], True, True, [invrow, posrow], [bk])
        for (tab, off) in ((sinT, 0.5), (cosT, 0.75)):
            ts("dve", yy[:, 0:n], bk[:, 0:n], 1.0 / (2 * math.pi), ALU.mult, [bk], [yy], s2=off, op1=ALU.add)
            cp("dve", yi[:, 0:n], yy[:, 0:n], [yy], [yi])
            cp("dve", yf[:, 0:n], yi[:, 0:n], [yi], [yf])
            tt("dve", yy[:, 0:n], yy[:, 0:n], yf[:, 0:n], ALU.subtract, [yy, yf], [yy])
            ts("dve", yf[:, 0:n], yy[:, 0:n], 0.0, ALU.is_lt, [yy], [yf])
            tt("dve", yy[:, 0:n], yy[:, 0:n], yf[:, 0:n], ALU.add, [yy, yf], [yy])
            ts("dve", yy[:, 0:n], yy[:, 0:n], 0.0, ALU.max, [yy], [yy], s2=0.9999999, op1=ALU.min)
            ts("dve", yy[:, 0:n], yy[:, 0:n], 2 * math.pi, ALU.mult, [yy], [yy], s2=-math.pi, op1=ALU.add)
            ts("dve", yy[:, 0:n], yy[:, 0:n], -3.1415925, ALU.max, [yy], [yy], s2=3.1415925, op1=ALU.min)
            act(tab[:, cs_], yy[:, 0:n], AF.Sin, [yy], [tab])

    T.dma("sp", ropetab[0], cosT[:], r=[cosT])
    T.dma("sp", ropetab[1], sinT[:], r=[sinT])
    T.pop()
    eps_n = T.sb([128, 1], F32, "eps_n")
    ms("pool", eps_n[:], NORM_EPS, [eps_n])
    rowm0 = T.sb([128, 1], F32, "rowm0")
    ms("pool", rowm0[:], 0.0, [rowm0])
    ms("pool", rowm0[0:1, :], 1.0, [rowm0])
    eps_g = T.sb([128, 1], F32, "eps_g")
    ms("pool", eps_g[:], GN_EPS, [eps_g])

    xtok = [[T.tok("x%d_%d" % (b, c)) for c in range(NCH)] for b in range(2)]

    def x_src(l, c):
        if c < 16:
            src = I["x_p"] if l == 0 else xbuf[(l - 1) % 2]
            return src[c * 128:(c + 1) * 128, :], 128
        src = I["x_s"] if l == 0 else xsb[(l - 1) % 2]
        return src[0:1, :], 1

    def y_dst(l, c):
        last = l == DEPTH - 1
        if c < 16:
            dst = O["y_p"] if last else xbuf[l % 2]
            return dst[c * 128:(c + 1) * 128, :], 128
        dst = O["y_s"] if last else xsb[l % 2]
        return dst[0:1, :], 1

    def fm_col0(ti):
        if ti < 12:
            return ti * 128
        if ti == 12:
            return 3072
        if ti < 25:
            return 3200 + (ti - 13) * 128
        return 6272 + (ti - 25) * 128

    def norm_chunk(xt, c, normw, junk, ssq):
        act(junk[:], xt[:], AF.Square, [xt], [junk, ssq], accum=ssq[:, 0:1])
        act(ssq[:, 1:2], ssq[:, 0:1], AF.Sqrt, [ssq, eps_n], [ssq], scale=1.0 / D, bias=eps_n[:, 0:1])
        T.op("dve", lambda E: E.reciprocal(out=ssq[:, 1:2], in_=ssq[:, 1:2]), r=[ssq], w=[ssq])
        ts("dve", xt[:], xt[:], ssq[:, 1:2], ALU.mult, [xt, ssq], [xt])
        for q in range(4):
            bk = nb()
            for k4 in range(4):
                kt = q * 4 + k4
                tp(bk[:, k4 * 128:(k4 + 1) * 128], xt[:, kt * 128:(kt + 1) * 128], identF[:], [xt, identF], [bk])
            tt("dve", big[:, q * 4:(q + 1) * 4, c * 128:(c + 1) * 128],
               bk[:].rearrange("p (a b) -> p a b", a=4), bc(normw[:, q * 4:(q + 1) * 4], [128, 4, 128], 2),
               ALU.mult, [bk, normw], [btok[c]])

    for l in range(DEPTH):
        T.push()
        if l == 0 or not FUSE_NORM:
            normw = T.sb([128, 16], F32, "normw")
            T.dma("sp", normw[:], I["norm_w"][l].rearrange("(k p) -> p k", p=128), w=[normw])
            xin = [T.sb([128, D], F32, "xin") for _ in range(2)]
            junk = T.sb([128, D], BF16, "junk")
            ssq = T.sb([128, 2], F32, "ssq")
            for c in CHS:
                xt = xin[c % 2]
                src, rows = x_src(l, c)
                rd = [xtok[(l - 1) % 2][c]] if l > 0 else []
                if rows == 1:
                    ms("pool", xt[:], 0.0, [xt])
                T.dma("sp", xt[0:rows, :], src, r=rd, w=[xt])
                norm_chunk(xt, c, normw, junk, ssq)
        T.pop()

        T.push()
        wf = [T.sb([128, 16, 128], BF16, "wf") for _ in range(2)]
        stg = [T.sb([128, 512], F32, "stg") for _ in range(4)]
        sti = [0]

        def nstg():
            sti[0] = (sti[0] + 1) % 4
            return stg[sti[0]]

        for ti in FMT:
            w_ = wf[ti % 2]
            c0 = fm_col0(ti)
            T.dma("pool", w_[:], I["w_in"][l][:, c0:c0 + 128].rearrange("(k p) c -> p k c", p=128), w=[w_])
            for tb in TBS:
                n = 512 if tb < 4 else 128
                bk = nb()
                for kt in range(16):
                    mm(bk[:, 0:n], w_[:, kt, :], big[:, kt, tb * 512:tb * 512 + n], kt == 0, kt == 15,
                       [w_] + btok[tb * 4:tb * 4 + (4 if tb < 4 else 1)], [bk])
                st = nstg()
                evac(st[:, 0:n], bk[:, 0:n], [bk], [st])
                T.dma("sp", zf[ti * 128:(ti + 1) * 128, tb * 512:tb * 512 + n], st[:, 0:n], r=[st])
                if ti <= 12 and tb == 3:
                    T.dma("sp", O["p_shift"][l, c0:c0 + 128].rearrange("(p o) -> p o", o=1), st[:, 511:512], r=[st])
                if ti <= 12 and tb == 4:
                    T.dma("sp", O["s_shift"][l, c0:c0 + 128].rearrange("(p o) -> p o", o=1), st[:, 0:1], r=[st])
        wt = [T.sb([128, 16, 512], BF16, "wt") for _ in range(2)]
        tmb = [(1536, 512, "vg", 0), (2048, 512, "vg", 512), (2560, 512, "vg", 1024),
               (4736, 256, "av", 0), (4992, 256, "av", 1), (5248, 256, "av", 2),
               (5504, 512, "ag", 0), (6016, 256, "ag", 512), (6272, 512, "pu", 0)]
        for bi_, (c0, ncol, kind, arg) in enumerate(tmb):
            if kind not in TMK:
                continue
            w_ = wt[bi_ % 2]
            T.dma("pool", w_[:, :, 0:ncol], I["w_in"][l][:, c0:c0 + ncol].rearrange("(k p) c -> p k c", p=128), w=[w_])
            for c in CHS:
                if kind == "pu" and c < 15:
                    continue
                if kind == "av":
                    start, step = blk_cols(arg, c)
                else:
                    start, step = c * 128, 1
                if step == 1:
                    toks = [btok[start // 128]]
                elif step == 4:
                    toks = btok[(start // 512) * 4:(start // 512) * 4 + 4]
                else:
                    toks = btok[0:16]
                bk = nb()
                for kt in range(16):
                    lhs = big[:, kt, start:start + 128] if step == 1 else big[:, kt, bass.ds(start, 128, step=step)]
                    mm(bk[:, 0:ncol], lhs, w_[:, kt, 0:ncol], kt == 0, kt == 15, [w_] + toks, [bk])
                st = nstg()
                evac(st[:, 0:ncol], bk[:, 0:ncol], [bk], [st])
                rows = slice(c * 128, (c + 1) * 128)
                if kind == "vg":
                    T.dma("sp", zt_vg[rows, arg:arg + ncol], st[:, 0:ncol], r=[st])
                    if c == 15:
                        T.dma("sp", O["p_shift"][l:l + 1, c0:c0 + ncol], st[127:128, 0:ncol], r=[st])
                    if c == 16:
                        T.dma("sp", O["s_shift"][l:l + 1, c0:c0 + ncol], st[0:1, 0:ncol], r=[st])
                elif kind == "ag":
                    T.dma("sp", zt_ag[rows, arg:arg + ncol], st[:, 0:ncol], r=[st])
                elif kind == "av":
                    g = arg
                    T.dma("sp", zt_av[g, rows, :], st[:, 0:256], r=[st])
                    if c == 16:
                        T.dma("sp", O["s_v%d" % g][l, 0:1, :], st[0:1, 0:256], r=[st])
                    elif g == 0 and c == 15:
                        T.dma("sp", O["p_v0"][l, :, :], st[:, 0:256], r=[st])
                    elif g == 1 and c >= 12:
                        dst = O["p_v1"][l].rearrange("(i d) c -> i d c", d=4)[:, c % 4, :]
                        T.dma("sp", dst, st[:, 0:256], r=[st])
                    elif g == 2:
                        dst = O["p_v2"][l].rearrange("(i d) c -> i d c", d=16)[:, c, :]
                        T.dma("sp", dst, st[:, 0:256], r=[st])
                elif kind == "pu":
                    if c == 15:
                        T.dma("sp", O["p_pool"][l, :, :], st[113:128, 0:512], r=[st])
                    else:
                        T.dma("sp", O["s_pool"][l, 14:15, :], st[0:1, 0:512], r=[st])
                        T.dma("sp", O["s_pool"][l, 0:14, :], I["st_pool"][l, 1:15, :])
        T.pop()

        T.push()
        mu_fm = T.sb([128, 13], F32, "mu_fm")
        T.dma("sp", mu_fm[:, 0:12], I["rwkv_mu"][l, 0:1536].rearrange("(t p) -> p t", p=128), w=[mu_fm])
        T.dma("sp", mu_fm[:, 12:13], I["rwkv_mu"][l, 3072:3200].rearrange("(p o) -> p o", o=1), w=[mu_fm])
        mu_tm = T.sb([128, 1536], F32, "mu_tm")
        T.dma("sp", mu_tm[:], I["rwkv_mu"][l, 1536:3072].partition_broadcast(128), w=[mu_tm])
        pr = T.sb([128, 5, 6], F32, "pr")
        for i_, nm in enumerate(("rwkv_w0", "rwkv_a0", "rwkv_k_k", "rwkv_k_a", "rwkv_r_k")):
            T.dma("sp", pr[:, i_, :], I[nm][l].rearrange("(t p) -> p t", p=128), w=[pr])
        omka = T.sb([128, 6], F32, "omka")
        ts("dve", omka[:], pr[:, 3, :], -1.0, ALU.mult, [pr], [omka], s2=1.0, op1=ALU.add)
        lnw = T.sb([128, 768], F32, "lnw")
        lnb = T.sb([128, 768], F32, "lnb")
        T.dma("sp", lnw[:], I["rwkv_ln_w"][l].partition_broadcast(128), w=[lnw])
        T.dma("sp", lnb[:], I["rwkv_ln_b"][l].partition_broadcast(128), w=[lnb])
        lw_ = T.sb([128, 768], BF16, "loraw")
        T.dma("pool", lw_[0:64, :], I["rwkv_w_up"][l], w=[lw_])
        lwa_tok = T.tok("lwa")
        T.dma("pool", lw_[64:128, :], I["rwkv_a_up"][l], w=[lwa_tok])
        Hst = T.sb([128, 6, 64], F32, "Hst")
        Hbf = T.sb([128, 6, 64], BF16, "Hbf")
        zfc = T.sb([128, 13, 129], F32, "zfc")
        zs = T.sb([128, 13, 128], F32, "zs")
        zt = T.sb([128, 1536], F32, "zt")
        ztp = T.sb([128, 1536], F32, "ztp")
        lor = T.sb([128, 128], BF16, "lor")
        sigw = T.sb([128, 6, 128], F32, "sigw")
        aa = T.sb([128, 6, 128], F32, "aa")
        kk = T.sb([128, 6, 128], F32, "kk")
        tA = T.sb([128, 6, 128], F32, "tA")
        tB = T.sb([128, 6, 128], F32, "tB")
        tK1 = T.sb([128, 6, 128], F32, "tK1")
        tK2 = T.sb([128, 6, 128], F32, "tK2")
        cs = T.sb([128, 6, 128], F32, "cs")
        enE = T.sb([128, 6, 128], F32, "enE")
        bhat = T.sb([128, 6, 128], BF16, "bhat")
        khat = T.sb([128, 6, 128], BF16, "khat")
        prodb = T.sb([128, 6, 128], BF16, "prodb")
        Ml = [T.sb([128, 12, 128], BF16, "Ml") for _ in range(2)]
        Xb = [T.sb([128, 768], BF16, "Xb") for _ in range(2)]
        Oo = T.sb([128, 12, 64], F32, "Oo")
        Os = T.sb([128, 12, 64], F32, "Os")
        st12 = T.sb([128, 4, 12], F32, "st12")
        mixa = T.sb([128, 768], BF16, "mixa")
        hout = T.sb([128, 3, 128], F32, "hout")

        class CS:
            pass
        sets = [CS(), CS()]
        s0 = sets[0]
        s0.rhat = T.sb([128, 6, 128], BF16, "rhat")
        s0.ahat = T.sb([128, 6, 128], BF16, "ahat")
        s0.Bt = T.sb([128, 6, 128], BF16, "Bt")
        s0.Kt = T.sb([128, 6, 128], BF16, "Kt")
        s0.vbf = T.sb([128, 768], BF16, "vbf")
        s0.sgb = T.sb([128, 768], BF16, "sgb")
        s0.Nl = [T.sb([128, 12, 128], BF16, "Nl") for _ in range(7)]
        s0.MakT = T.sb([128, 12, 128], BF16, "MakT")
        s0.MrbT = T.sb([128, 12, 128], BF16, "MrbT")
        s0.MrkT = T.sb([128, 12, 128], BF16, "MrkT")
        flat = big[:, 6:16, :].rearrange("p a b -> p (a b)")
        off = [0]

        def carve(shape):
            n = int(np.prod(shape))
            v = flat[:, off[0]:off[0] + n]
            off[0] += n
            if len(shape) == 2:
                v = v.rearrange("p (a b) -> p a b", a=shape[0])
            return Buf(v, "carve")
        s1 = sets[1]
        s1.rhat = carve([6, 128])
        s1.ahat = carve([6, 128])
        s1.Bt = carve([6, 128])
        s1.Kt = carve([6, 128])
        s1.vbf = carve([768])
        s1.sgb = carve([768])
        s1.Nl = [carve([12, 128]) for _ in range(7)]
        s1.MakT = carve([12, 128])
        s1.MrbT = carve([12, 128])
        s1.MrkT = carve([12, 128])
        assert off[0] <= 10 * NT
        for S_ in sets:
            S_.wc = T.sb([128, 6], F32, "wc")
            S_.bon = T.sb([128, 12], F32, "bon")

        def sl_info(sl):
            j, par = sl % 6, sl // 6
            return j, par, slice(par * 64, par * 64 + 64), 2 * j + par

        utoks = {}

        def U(buf, slots):
            k = id(buf)
            if k not in utoks:
                utoks[k] = [T.tok("u") for _ in range(6)]
            return [utoks[k][u] for u in sorted(set(sl // 2 for sl in slots))]

        def XH(buf, half):
            k = (id(buf), "x")
            if k not in utoks:
                utoks[k] = [T.tok("xh") for _ in range(2)]
            return utoks[k][half]

        pbp = [0]
        pbs = [0]

        def nbp():
            pbp[0] = (pbp[0] + 1) % 4
            return pb[pbp[0]]

        def nbs():
            pbs[0] = (pbs[0] + 1) % 4
            return pb[4 + pbs[0]]

        def gen_prep(gc, first, sample, S_):
            col0 = gc * 128
            zsrc = zf[0:13 * 128, :].rearrange("(t p) n -> p t n", p=128)
            if first:
                T.dma("sp", zfc[:, :, 1:129], zsrc[:, :, col0:col0 + 128], w=[zfc])
                if sample:
                    T.dma("sp", zfc[:, 0:12, 0:1], I["st_shift"][l, 0:1536].rearrange("(t p o) -> p t o", p=128, o=1), w=[zfc])
                    T.dma("sp", zfc[:, 12, 0:1], I["st_shift"][l, 3072:3200].rearrange("(p o) -> p o", o=1), w=[zfc])
                else:
                    ms("pool", zfc[:, :, 0:1], 0.0, [zfc])
            else:
                T.dma("sp", zfc[:], zsrc[:, :, col0 - 1:col0 + 128], w=[zfc])
            T.dma("sp", zt[:], zt_vg[col0:col0 + 128, :], w=[zt])
            if first:
                T.dma("sp", ztp[1:128, :], zt_vg[col0:col0 + 127, :], w=[ztp])
                if sample:
                    T.dma("sp", ztp[0:1, :], I["st_shift"][l:l + 1, 1536:3072], w=[ztp])
                else:
                    ms("pool", ztp[0:1, :], 0.0, [ztp])
            else:
                T.dma("sp", ztp[:], zt_vg[col0 - 1:col0 + 127, :], w=[ztp])
            yield
            tt("dve", zs[:], zfc[:, :, 0:128], zfc[:, :, 1:129], ALU.subtract, [zfc], [zs])
            tt("dve", zs[:], zs[:], bc(mu_fm[:], [128, 13, 128], 2), ALU.mult, [zs, mu_fm], [zs])
            tt("dve", zs[:], zs[:], zfc[:, :, 1:129], ALU.add, [zs, zfc], [zs])
            yield
            if sample:
                ms("dve", zs[:, :, 1:128], 0.0, [zs])

            def child_tm():
                tt("dve", ztp[:], ztp[:], zt[:], ALU.subtract, [ztp, zt], [ztp])
                tt("dve", ztp[:], ztp[:], mu_tm[:], ALU.mult, [ztp, mu_tm], [ztp])
                yield
                tt("dve", zt[:], zt[:], ztp[:], ALU.add, [zt, ztp], [zt])
                if sample:
                    ts("dve", zt[:], zt[:], rowm0[:, 0:1], ALU.mult, [zt, rowm0], [zt])
                yield
                cp("act", S_.vbf[:], zt[:, 0:768], [zt], [S_.vbf])
                act(S_.sgb[:], zt[:, 768:1536], AF.Silu, [zt], [S_.sgb])
                yield

            def child_loc():
                act(lor[0:64, :], zs[0:64, 12, :], AF.Tanh, [zs], [lor])
                cp("dve", lor[64:128, :], zs[64:128, 12, :], [zs], [lor])
                for _ in range(LORA_DELAY):
                    yield
                bw = [pb[0], pb[1]]
                for j in range(6):
                    mm(bw[j // 4][:, (j % 4) * 128:(j % 4 + 1) * 128], lw_[0:64, j * 128:(j + 1) * 128], lor[0:64, :],
                       True, True, [lw_, lor], [bw[j // 4]])
                yield
                for j in range(6):
                    act(sigw[:, j, :], bw[j // 4][:, (j % 4) * 128:(j % 4 + 1) * 128], AF.Sigmoid, [bw[j // 4], pr], [sigw],
                        bias=pr[:, 0, j:j + 1])
                    if j % 2:
                        yield
                if sample:
                    ms("dve", sigw[:, :, 1:128], 0.0, [sigw])
                for j in range(6):
                    T.op("dve", lambda E, j=j: E.tensor_tensor_scan(out=cs[:, j, :], data0=ones[:], data1=sigw[:, j, :],
                                                                   initial=0.0, op0=ALU.mult, op1=ALU.add),
                         r=[ones, sigw], w=[cs])
                    if j % 2:
                        yield
                for j in range(6):
                    mm(bw[j // 4][:, (j % 4) * 128:(j % 4 + 1) * 128], lw_[64:128, j * 128:(j + 1) * 128], lor[64:128, :],
                       True, True, [lwa_tok, lor], [bw[j // 4]])
                yield
                for j in range(6):
                    act(aa[:, j, :], bw[j // 4][:, (j % 4) * 128:(j % 4 + 1) * 128], AF.Sigmoid, [bw[j // 4], pr], [aa],
                        bias=pr[:, 1, j:j + 1])
                    if j % 2:
                        yield
                act(tB[:], cs[:], AF.Exp, [cs], [tB], scale=-C0)
                tt("dve", tA[:], cs[:], sigw[:], ALU.subtract, [cs, sigw], [tA])
                yield
                act(tA[:], tA[:], AF.Exp, [tA], [tA], scale=-C0)
                act(enE[:], cs[:], AF.Exp, [cs], [enE], scale=C0)
                yield
                cp("dve", S_.wc[:], tB[:, :, 127], [tB], [S_.wc])
                tt("dve", S_.rhat[:], zs[:, 0:6, :], tB[:], ALU.mult, [zs, tB], [S_.rhat])
                yield

            def child_k():
                tt("dve", kk[:], zs[:, 6:12, :], bc(pr[:, 2, :], [128, 6, 128], 2), ALU.mult, [zs, pr], [kk])
                act(tK1[:], kk[:], AF.Square, [kk], [tK1])
                yield
                bq = [pb[2], pb[3]]
                tAf = tK1[:].rearrange("p a b -> p (a b)")
                mm(bq[0][:], bones[:], tAf[:, 0:512], True, True, [bones, tK1], [bq[0]])
                mm(bq[1][:, 0:256], bones[:], tAf[:, 512:768], True, True, [bones, tK1], [bq[1]])
                yield
                tBf = tK2[:].rearrange("p a b -> p (a b)")
                ts("dve", tBf[:, 0:512], bq[0][:], 1e-18, ALU.max, [bq[0]], [tK2])
                ts("dve", tBf[:, 512:768], bq[1][:, 0:256], 1e-18, ALU.max, [bq[1]], [tK2])
                yield
                act(tK2[:], tK2[:], AF.Ln, [tK2], [tK2])
                act(tK2[:], tK2[:], AF.Exp, [tK2], [tK2], scale=-0.5)
                yield
                tt("dve", kk[:], kk[:], tK2[:], ALU.mult, [kk, tK2], [kk])
                yield

            kids = [child_loc(), child_k(), child_tm()]
            while kids:
                for k_ in list(kids):
                    try:
                        next(k_)
                    except StopIteration:
                        kids.remove(k_)
                yield
            stt("dve", S_.ahat[:], kk[:], -1.0, tA[:], ALU.mult, ALU.mult, [kk, tA], [S_.ahat])
            yield
            tt("dve", tA[:], kk[:], aa[:], ALU.mult, [kk, aa], [tA])
            tt("dve", bhat[:], tA[:], enE[:], ALU.mult, [tA, enE], [bhat])
            for j in range(6):
                act(tB[:, j, :], aa[:, j, :], AF.Identity, [aa, pr, omka], [tB], scale=pr[:, 3, j:j + 1], bias=omka[:, j:j + 1])
            yield
            tt("dve", tB[:], tB[:], zs[:, 6:12, :], ALU.mult, [tB, zs], [tB])
            yield
            tt("dve", khat[:], tB[:], enE[:], ALU.mult, [tB, enE], [khat])
            tt("dve", tA[:], tB[:], zs[:, 0:6, :], ALU.mult, [tB, zs], [tA])
            tt("dve", prodb[:], tA[:], bc(pr[:, 4, :], [128, 6, 128], 2), ALU.mult, [tA, pr], [prodb])
            yield
            bk = nbp()
            for j in range(6):
                mm(bk[:, 2 * j:2 * j + 2], prodb[:, j, :], bsel[:], True, True, [prodb, bsel], [bk])
            cp("dve", S_.bon[:], bk[:, 0:12], [bk], [S_.bon])
            yield
            for (srcb, dstb) in ((bhat, S_.Bt), (khat, S_.Kt)):
                bk = nbp()
                bkb = bk[:].bitcast(BF16)
                for j in range(6):
                    tp(bkb[:, j * 128:(j + 1) * 128], srcb[:, j, :], identB[:], [srcb, identB], [bk])
                evac(dstb[:], bkb[:, 0:768].rearrange("p (a b) -> p a b", a=6), [bk], [dstb])
                yield
            specs = ((bhat, S_.ahat, m_su, S_.Nl[0]), (S_.ahat, bhat, m_sl, Ml[0]), (khat, S_.ahat, m_su, S_.MakT),
                     (bhat, S_.rhat, m_u, S_.MrbT), (khat, S_.rhat, m_u, S_.MrkT))
            for (L_, R_, mk, dst) in specs:
                for grp in ((0, 1, 2, 3), (4, 5), (6, 7, 8, 9), (10, 11)):
                    bk = nbp()
                    n_ = len(grp)
                    for idx, sl in enumerate(grp):
                        j, par = sl % 6, sl // 6
                        ps_ = slice(par * 64, par * 64 + 64)
                        mm(bk[:, idx * 128:(idx + 1) * 128], L_[ps_, j, :], R_[ps_, j, :], True, True, [L_, R_], [bk])
                    dv_ = dst[:, grp[0]:grp[0] + n_, :]
                    cp("act", dv_, bk[:, 0:n_ * 128].rearrange("p (a b) -> p a b", a=n_), [bk], U(dst, grp))
                    tt("dve", dv_, dv_, mk[:, 0:n_, :], ALU.mult, U(dst, grp) + [mk], U(dst, grp))
                    yield
            for i in range(1, 7):
                if sample:
                    break
                Np, Mp = S_.Nl[i - 1], Ml[(i - 1) % 2]
                for hg in range(3):
                    bk = nbp()
                    for hh in range(4):
                        h = hg * 4 + hh
                        mm(bk[:, hh * 128:(hh + 1) * 128], Mp[:, h, :], Np[:, h, :], True, True, U(Mp, [h]) + U(Np, [h]), [bk])
                    evac(S_.Nl[i][:, hg * 4:(hg + 1) * 4, :], bk[:].rearrange("p (a b) -> p a b", a=4), [bk],
                         U(S_.Nl[i], range(hg * 4, hg * 4 + 4)))
                    yield
                if i < 6:
                    for hg in range(3):
                        bk = nbp()
                        for hh in range(4):
                            h = hg * 4 + hh
                            mm(bk[:, hh * 128:(hh + 1) * 128], Np[:, h, :], Mp[:, h, :], True, True, U(Mp, [h]) + U(Np, [h]), [bk])
                        evac(Ml[i % 2][:, hg * 4:(hg + 1) * 4, :], bk[:].rearrange("p (a b) -> p a b", a=4), [bk],
                             U(Ml[i % 2], range(hg * 4, hg * 4 + 4)))
                        yield

        def gen_seq(gc, S_, sample=False):
            col0 = gc * 128
            ahat, rhat, vbf, Nl = S_.ahat, S_.rhat, S_.vbf, S_.Nl
            xa, xb_ = nbs(), nbs()
            for sl in range(12):
                j, par, ps_, h = sl_info(sl)
                bk = xa if par == 0 else xb_
                dst = bk[:, j * 64:(j + 1) * 64]
                mm(dst, ahat[ps_, j, :], Hbf[ps_, j, :], True, False, [ahat, Hbf], [bk])
                mm(dst, S_.MakT[:, sl, :], vbf[:, h * 64:(h + 1) * 64], False, True, U(S_.MakT, [sl]) + [vbf], [bk])
            Xc = Xb[0]
            cp("act", Xc[:, 0:384], xa[:, 0:384], [xa], [XH(Xc, 0)])
            cp("dve", Xc[:, 384:768], xb_[:, 0:384], [xb_], [XH(Xc, 1)])
            yield
            for i in range(7):
                if sample:
                    break
                xa, xb_ = nbs(), nbs()
                for sl in range(12):
                    j, par, ps_, h = sl_info(sl)
                    bk = xa if par == 0 else xb_
                    dst = bk[:, j * 64:(j + 1) * 64]
                    mm(dst, identB[:], Xc[:, sl * 64:(sl + 1) * 64], True, False, [identB, XH(Xc, par)], [bk])
                    mm(dst, Nl[i][:, sl, :], Xc[:, sl * 64:(sl + 1) * 64], False, True, U(Nl[i], [sl]) + [XH(Xc, par)], [bk])
                Xn = Xb[(i + 1) % 2]
                cp("act", Xn[:, 0:384], xa[:, 0:384], [xa], [XH(Xn, 0)])
                cp("dve", Xn[:, 384:768], xb_[:, 0:384], [xb_], [XH(Xn, 1)])
                Xc = Xn
                yield
            Uu = Xc
            oa_, ob_ = nbs(), nbs()
            for sl in range(12):
                j, par, ps_, h = sl_info(sl)
                bk = oa_ if par == 0 else ob_
                dst = bk[:, j * 64:(j + 1) * 64]
                mm(dst, rhat[ps_, j, :], Hbf[ps_, j, :], True, False, [rhat, Hbf], [bk])
                mm(dst, S_.MrbT[:, sl, :], Uu[:, sl * 64:(sl + 1) * 64], False, False, U(S_.MrbT, [sl]) + [XH(Uu, par)], [bk])
                mm(dst, S_.MrkT[:, sl, :], vbf[:, h * 64:(h + 1) * 64], False, True, U(S_.MrkT, [sl]) + [vbf], [bk])
            Oof = Oo[:].rearrange("p a b -> p (a b)")
            evac(Oo[:, bass.ds(0, 6, step=2), :], oa_[:, 0:384].rearrange("p (a b) -> p a b", a=6), [oa_], [Oo])
            evac(Oo[:, bass.ds(1, 6, step=2), :], ob_[:, 0:384].rearrange("p (a b) -> p a b", a=6), [ob_], [Oo])
            yield
            ha, hb = nbs(), nbs()
            for j in range(6):
                bk = ha if j < 4 else hb
                dst = bk[:, (j % 4) * 128:(j % 4 + 1) * 128]
                mm(dst, S_.Bt[:, j, :], Uu[:].rearrange("p (par j e) -> p par j e", par=2, j=6)[:, :, j, :], True, False,
                   [S_.Bt, XH(Uu, 0), XH(Uu, 1)], [bk])
                mm(dst, S_.Kt[:, j, :], vbf[:, j * 128:(j + 1) * 128], False, True, [S_.Kt, vbf], [bk])
            for par in range(2):
                ps_ = slice(par * 64, par * 64 + 64)
                tt("dve", Hst[ps_, 0:4, :], ha[ps_, :].rearrange("p (a b) -> p a b", a=4)[:, :, par * 64:par * 64 + 64],
                   Hst[ps_, 0:4, :], ALU.add, [ha, Hst], [Hst])
                tt("dve", Hst[ps_, 4:6, :], hb[ps_, 0:256].rearrange("p (a b) -> p a b", a=2)[:, :, par * 64:par * 64 + 64],
                   Hst[ps_, 4:6, :], ALU.add, [hb, Hst], [Hst])
                tt("dve", Hst[ps_, :, :], Hst[ps_, :, :], bc(S_.wc[ps_, :], [64, 6, 64], 2), ALU.mult, [Hst, S_.wc], [Hst])
                cp("act", Hbf[ps_, :, :], Hst[ps_, :, :], [Hst], [Hbf])
            yield
            T.op("dve", lambda E: E.tensor_reduce(out=st12[:, 0, :], in_=Oo[:], axis=AX.X, op=ALU.add), r=[Oo], w=[st12])
            act(Os[:], Oo[:], AF.Square, [Oo], [Os])
            T.op("dve", lambda E: E.tensor_reduce(out=st12[:, 1, :], in_=Os[:], axis=AX.X, op=ALU.add), r=[Os], w=[st12])
            ts("dve", st12[:, 0, :], st12[:, 0, :], 1.0 / 64, ALU.mult, [st12], [st12])
            tt("dve", st12[:, 2, :], st12[:, 0, :], st12[:, 0, :], ALU.mult, [st12], [st12])
            stt("dve", st12[:, 1, :], st12[:, 1, :], 1.0 / 64, st12[:, 2, :], ALU.mult, ALU.subtract, [st12], [st12])
            act(st12[:, 3, :], st12[:, 1, :], AF.Sqrt, [st12, eps_g], [st12], bias=eps_g[:, 0:1])
            T.op("dve", lambda E: E.reciprocal(out=st12[:, 3, :], in_=st12[:, 3, :]), r=[st12], w=[st12])
            yield
            tt("dve", Os[:], Oo[:], bc(st12[:, 0, :], [128, 12, 64], 2), ALU.subtract, [Oo, st12], [Os])
            tt("dve", Os[:], Os[:], bc(st12[:, 3, :], [128, 12, 64], 2), ALU.mult, [Os, st12], [Os])
            Osf = Os[:].rearrange("p a b -> p (a b)")
            tt("dve", Osf, Osf, lnw[:], ALU.mult, [Os, lnw], [Os])
            tt("dve", Osf, Osf, lnb[:], ALU.add, [Os, lnb], [Os])
            yield
            tt("dve", Oo[:], vbf[:].rearrange("p (a b) -> p a b", a=12), bc(S_.bon[:], [128, 12, 64], 2), ALU.mult,
               [vbf, S_.bon], [Oo])
            tt("dve", Os[:], Os[:], Oo[:], ALU.add, [Os, Oo], [Os])
            tt("dve", mixa[:], Osf, S_.sgb[:], ALU.mult, [Os, S_.sgb], [mixa])
            bk = nbs()
            bkb = bk[:].bitcast(BF16)
            for j in range(6):
                tp(bkb[:, j * 128:(j + 1) * 128], mixa[:, j * 128:(j + 1) * 128], identB[:], [mixa, identB], [bk])
            evac(big[:, 0:6, col0:col0 + 128], bkb[:, 0:768].rearrange("p (a b) -> p a b", a=6), [bk], [btok[gc]])
            yield

        def run_gens(gens):
            active = list(gens)
            while active:
                for it_ in list(active):
                    g_, w_ = it_
                    for _ in range(w_):
                        try:
                            next(g_)
                        except StopIteration:
                            active.remove(it_)
                            break

        def rwkv_seq(chunks, sample):
            if not sample:
                ms("dve", Hst[:], 0.0, [Hst])
                ms("dve", Hbf[:], 0.0, [Hbf])
            else:
                sv = I["st_wkv"][l].rearrange("(jp jj par) v k -> jj v jp par k", jp=3, jj=2, par=2)
                for jj in range(2):
                    for jp in range(3):
                        T.dma("sp", hout[jj * 64:(jj + 1) * 64, jp, :].rearrange("p (par k) -> p par k", par=2), sv[jj, :, jp], w=[hout])
                bk = nb()
                for jp in range(3):
                    tp(bk[:, jp * 128:(jp + 1) * 128], hout[:, jp, :], identF[:], [hout, identF], [bk])
                cp("dve", Hst[:].rearrange("p a b -> p (a b)"), bk[:, 0:384], [bk], [Hst])
                cp("act", Hbf[:], Hst[:], [Hst], [Hbf])
            n_ = len(chunks)
            for ci in range(n_ + 1):
                gens = []
                if ci > 0:
                    gens.append((gen_seq(chunks[ci - 1], sets[(ci - 1) % 2], sample), 1))
                if ci < n_:
                    gens.append((gen_prep(chunks[ci], ci == 0, sample, sets[ci % 2]), PREP_W))
                run_gens(gens)
            dv = O["s_wkv" if sample else "p_wkv"][l].rearrange("(jp jj par) v k -> jj v jp par k", jp=3, jj=2, par=2)
            bk = nb()
            Hf = Hst[:].rearrange("p a b -> p (a b)")
            for jp in range(3):
                tp(bk[:, jp * 128:(jp + 1) * 128], Hf[:, jp * 128:(jp + 1) * 128], identF[:], [Hst, identF], [bk])
            evac(hout[:].rearrange("p a b -> p (a b)"), bk[:, 0:384], [bk], [hout])
            for jj in range(2):
                for jp in range(3):
                    T.dma("sp", dv[jj, :, jp], hout[jj * 64:(jj + 1) * 64, jp, :].rearrange("p (par k) -> p par k", par=2), r=[hout])

        if "p2" in phases:
            rwkv_seq(PCH, False)
            rwkv_seq([16], True)
        T.pop()

        T.push()
        qT = T.sb([128, 6, NT], BF16, "qT")
        kT = T.sb([128, 6, NT], BF16, "kT")
        nw = T.sb([128, 2], F32, "nw")
        cosT = T.sb([128, NT], F32, "cosT")
        sinT = T.sb([128, NT], F32, "sinT")
        T.dma("sp", cosT[:], ropetab[0], w=[cosT])
        T.dma("sp", sinT[:], ropetab[1], w=[sinT])
        for hb_ in range(2):
            T.dma("sp", nw[hb_ * 64:(hb_ + 1) * 64, 0:1], I["q_norm_w"][l].rearrange("(p o) -> p o", o=1), w=[nw])
            T.dma("sp", nw[hb_ * 64:(hb_ + 1) * 64, 1:2], I["k_norm_w"][l].rearrange("(p o) -> p o", o=1), w=[nw])
        ts("dve", nw[:, 0:1], nw[:, 0:1], 0.125, ALU.mult, [nw], [nw])

        pipe_depth = [2]

        def slot_banks(slot):
            st = [0]
            nper = 8 // pipe_depth[0]

            def f():
                st[0] = (st[0] + 1) % nper
                return pb[slot * nper + st[0]]
            return f

        def run_pipe(factories, depth=2):
            pipe_depth[0] = depth
            pending = list(factories)
            active = {}
            while pending or active:
                for sl_ in range(depth):
                    if sl_ not in active and pending:
                        active[sl_] = pending.pop(0)(sl_)
                for sl_ in list(active.keys()):
                    try:
                        next(active[sl_])
                    except StopIteration:
                        del active[sl_]

        if "p3" in phases:
            T.push()
            QD = 4
            zq = [T.sb([128, 512], F32, "zq") for _ in range(QD)]
            sq = [T.sb([128, 512], F32, "sq") for _ in range(QD)]
            rs = [T.sb([128, 512], F32, "rs") for _ in range(QD)]
            zn = [T.sb([128, 512], F32, "zn") for _ in range(QD)]
            kfin = [T.sb([128, 512], F32, "kfin") for _ in range(QD)]
            kout = [T.sb([128, 128], F32, "kout") for _ in range(QD)]

            def qk_block(ti, tb):
                def g_(slot):
                    nbk = slot_banks(slot)
                    isk = ti >= 6
                    n = 512 if tb < 4 else 128
                    cs_ = slice(tb * 512, tb * 512 + n)
                    z_, sq_, rs_, zn_, kf_, ko = zq[slot], sq[slot], rs[slot], zn[slot], kfin[slot], kout[slot]
                    T.dma("sp", z_[:, 0:n], zf[(13 + ti) * 128:(14 + ti) * 128, cs_], w=[z_])
                    yield
                    tt("pool", sq_[:, 0:n], z_[:, 0:n], z_[:, 0:n], ALU.mult, [z_], [sq_])
                    bk = nbk()
                    mm(bk[:, 0:n], bones[:], sq_[:, 0:n], True, True, [bones, sq_], [bk])
                    yield
                    act(rs_[:, 0:n], bk[:, 0:n], AF.Ln, [bk, eps_n], [rs_], scale=1.0 / 64, bias=eps_n[:, 0:1])
                    act(rs_[:, 0:n], rs_[:, 0:n], AF.Exp, [rs_], [rs_], scale=-0.5)
                    yield
                    stt("dve", zn_[:, 0:n], z_[:, 0:n], nw[:, (1 if isk else 0):(2 if isk else 1)], rs_[:, 0:n], ALU.mult,
                        ALU.mult, [z_, nw, rs_], [zn_])
                    bk2 = nbk()
                    mm(bk2[:, 0:n], prot[:], zn_[:, 0:n], True, True, [prot, zn_], [bk2])
                    yield
                    tt("dve", sq_[:, 0:n], zn_[:, 0:n], cosT[:, cs_], ALU.mult, [zn_, cosT], [sq_])
                    tt("dve", rs_[:, 0:n], bk2[:, 0:n], sinT[:, cs_], ALU.mult, [bk2, sinT], [rs_])
                    yield
                    if not isk:
                        tt("dve", qT[:, ti, cs_], sq_[:, 0:n], rs_[:, 0:n], ALU.add, [sq_, rs_], [qT])
                        yield
                    else:
                        tt("dve", kf_[:, 0:n], sq_[:, 0:n], rs_[:, 0:n], ALU.add, [sq_, rs_], [kf_])
                        cp("act", kT[:, ti - 6, cs_], kf_[:, 0:n], [kf_], [kT])
                        yield
                        g, half = (ti - 6) // 2, (ti - 6) % 2
                        for cc in range(n // 128):
                            gc = tb * 4 + cc
                            need = (gc == 16) or (g == 0 and gc == 15) or (g == 1 and gc >= 12) or g == 2
                            if not need:
                                continue
                            bk3 = nbk()
                            tp(bk3[:, 0:128], kf_[:, cc * 128:(cc + 1) * 128], identF[:], [kf_, identF], [bk3])
                            evac(ko[:], bk3[:, 0:128], [bk3], [ko])
                            hc = slice(half * 128, half * 128 + 128)
                            if gc == 16:
                                T.dma("sp", O["s_k%d" % g][l, 0:1, hc], ko[0:1, :], r=[ko])
                            else:
                                keep = (128, 512, 2048)[g]
                                r0 = gc * 128 - (2048 - keep)
                                T.dma("sp", O["p_k%d" % g][l, r0:r0 + 128, hc], ko[:], r=[ko])
                            yield
                return g_

            run_pipe([qk_block(ti, tb) for ti in range(12) for tb in range(5)], depth=QD)
            T.pop()
            T.push()
            kTc = T.sb([128, 3, 2, 128], BF16, "kTc")
            Va = T.sb([128, 3, 17, 4, 65], BF16, "Va")
            Vc = T.sb([128, 3, 4, 65], BF16, "Vc")
            ms("pool", Va[:], 1.0, [Va])
            ms("pool", Vc[:], 1.0, [Vc])
            vst = [T.sb([128, 256], F32, "vst") for _ in range(6)]
            it = 0
            for g in range(3):
                for bi in range(17):
                    v_ = vst[it % 6]
                    it += 1
                    T.dma("sp", v_[:], zt_av[g, bi * 128:(bi + 1) * 128, :], w=[v_])
                    cp("act" if it % 2 else "dve", Va[:, g, bi, :, 0:64], v_[:].rearrange("p (a b) -> p a b", a=4), [v_], [Va])
                d = DILS[g]
                v_ = vst[it % 6]
                it += 1
                T.dma("sp", v_[:], I["cv%d" % g][l].rearrange("(r d) c -> r d c", d=d)[:, 0, :], w=[v_])
                cp("dve", Vc[:, g, :, 0:64], v_[:].rearrange("p (a b) -> p a b", a=4), [v_], [Vc])
                v_ = vst[it % 6]
                it += 1
                T.dma("sp", v_[:], I["ck%d" % g][l].rearrange("(r d) c -> r d c", d=d)[:, 0, :], w=[v_])
                for half in range(2):
                    bk = nb()
                    tp(bk[:, 0:128], v_[:, half * 128:(half + 1) * 128], identF[:], [v_, identF], [bk])
                    evac(kTc[:, g, half, :], bk[:, 0:128], [bk], [kTc])
            pT = [T.sb([128, 2, 2, 2, 128], BF16, "pT") for _ in range(2)]
            ost = [T.sb([128, 260], F32, "ost") for _ in range(2)]
            csl = (lambda s0, st_: slice(s0, s0 + 128) if st_ == 1 else bass.ds(s0, 128, step=st_))

            def att_block(g, bi):
                def g_(slot):
                    nbk = slot_banks(slot)
                    start, step = blk_cols(g, bi)
                    prev = blk_prev(g, bi)
                    cur = csl(start, step)
                    p_ = pT[slot]
                    o_ = ost[slot]
                    for h2 in range(2):
                        bk = nbk()
                        bkv = bk[:].rearrange("p (a b c) -> p a b c", a=2, b=2)
                        ps_ = slice(h2 * 64, h2 * 64 + 64)
                        for half in range(2):
                            j = g * 2 + half
                            if prev is not None:
                                if prev == "cache":
                                    lhs = kTc[ps_, g, half, :]
                                else:
                                    ps0, pst = blk_cols(g, prev)
                                    lhs = kT[ps_, j, csl(ps0, pst)]
                                mm(bkv[:, half, 0, :], lhs, qT[ps_, j, cur], True, True, [kT, kTc, qT], [bk])
                            mm(bkv[:, half, 1, :], kT[ps_, j, cur], qT[ps_, j, cur], True, True, [kT, qT], [bk])
                        yield
                        if prev is not None:
                            act(p_[:, :, h2], bkv, AF.Exp, [bk], [p_])
                            tt("dve", p_[:, :, h2], p_[:, :, h2], amask[:], ALU.mult, [p_, amask], [p_])
                        else:
                            act(p_[:, :, h2, 1, :], bkv[:, :, 1, :], AF.Exp, [bk], [p_])
                            tt("dve", p_[:, :, h2, 1, :], p_[:, :, h2, 1, :], amask[:, :, 1, :], ALU.mult, [p_, amask], [p_])
                        yield
                    bo = nbk()
                    for hl in range(4):
                        half, h2 = hl // 2, hl % 2
                        dst = bo[:, hl * 65:(hl + 1) * 65]
                        if prev is not None:
                            rv = Vc[:, g, hl, :] if prev == "cache" else Va[:, g, prev, hl, :]
                            mm(dst, p_[:, half, h2, 0, :], rv, True, False, [p_, Va, Vc], [bo])
                        mm(dst, p_[:, half, h2, 1, :], Va[:, g, bi, hl, :], prev is None, True, [p_, Va], [bo])
                    yield
                    evac(o_[:], bo[:, 0:260], [bo], [o_])
                    if bi == 16:
                        dst = oacc[g, 2048:NT, :]
                    else:
                        d = DILS[g]
                        i0 = (start - start % d) // d
                        dst = oacc[g, 0:2048, :].rearrange("(i d) c -> i d c", d=d)[i0:i0 + 128, start % d, :]
                    T.dma("sp", dst, o_[:], r=[o_])
                    yield
                return g_

            run_pipe([att_block(g, bi) for g in range(3) for bi in range(17)])
            T.pop()
            T.pop()
            T.push()
            oa3 = [T.sb([128, 3, 260], F32, "oa3") for _ in range(2)]
            zg = [T.sb([128, 768], F32, "zg") for _ in range(2)]
            ls = [T.sb([128, 4], F32, "ls") for _ in range(2)]
            mixb = [T.sb([128, 3, 4, 64], F32, "mixb") for _ in range(2)]
            mixbb = [T.sb([128, 768], BF16, "mixbb") for _ in range(2)]

            def comb_block(gc):
                def g_(slot):
                    nbk = slot_banks(slot)
                    o3, gz, ls_, mb, mbb = oa3[slot], zg[slot], ls[slot], mixb[slot], mixbb[slot]
                    T.dma("sp", o3[:], oacc[:, gc * 128:(gc + 1) * 128, :].rearrange("g t c -> t g c"), w=[o3])
                    T.dma("sp", gz[:], zt_ag[gc * 128:(gc + 1) * 128, :], w=[gz])
                    yield
                    o3v = o3[:].rearrange("p g (h e) -> p g h e", h=4)
                    tt("dve", ls_[:], o3v[:, 0, :, 64], o3v[:, 1, :, 64], ALU.add, [o3], [ls_])
                    tt("dve", ls_[:], ls_[:], o3v[:, 2, :, 64], ALU.add, [ls_, o3], [ls_])
                    T.op("dve", lambda E: E.reciprocal(out=ls_[:], in_=ls_[:]), r=[ls_], w=[ls_])
                    act(gz[:], gz[:], AF.Silu, [gz], [gz])
                    yield
                    for g in range(3):
                        tt("dve", mb[:, g], o3v[:, g, :, 0:64], bc(ls_[:], [128, 4, 64], 2), ALU.mult, [o3, ls_], [mb])
                    yield
                    tt("dve", mbb[:], mb[:].rearrange("p g h e -> p (g h e)"), gz[:], ALU.mult, [mb, gz], [mbb])
                    bk = nbk()
                    bkb = bk[:].bitcast(BF16)
                    for j in range(6):
                        tp(bkb[:, j * 128:(j + 1) * 128], mbb[:, j * 128:(j + 1) * 128], identB[:], [mbb, identB], [bk])
                    yield
                    evac(big[:, 6:12, gc * 128:(gc + 1) * 128], bkb[:, 0:768].rearrange("p (a b) -> p a b", a=6), [bk], [btok[gc]])
                    yield
                return g_

            run_pipe([comb_block(gc) for gc in range(NCH)])
        T.pop()

        T.push()
        wo = [T.sb([128, 16, 512], BF16, "wo") for _ in range(4)]
        if "p5" in phases:
            for cb in range(4):
                T.dma("pool", wo[cb][:], I["w_out"][l][:, cb * 512:(cb + 1) * 512].rearrange("(k p) c -> p k c", p=128), w=[wo[cb]])
        T.push()
        if "p4" in phases:
            pw = T.sb([128, 4, 128], BF16, "pw")
            T.dma("pool", pw[:], I["pool_w"][l].rearrange("g c d -> c g d"), w=[pw])
            psc = T.sb([128, 4], F32, "psc")
            T.dma("sp", psc[:], I["pool_scale"][l].rearrange("(g p) -> p g", p=128), w=[psc])
            WB = 16 + 2048
            ub = T.sb([128, WB], F32, "ub")
            sA = T.sb([128, WB], F32, "sA")
            sB = T.sb([128, WB], F32, "sB")
            db = T.sb([128, 2048], BF16, "db")
            gt = T.sb([128, 2048], F32, "gt")
            for gi in range(4):
                wsz = 2 ** (gi + 1)
                for sample in (False, True):
                    Wn = WB if not sample else 17
                    nreal = 2048 if not sample else 1
                    ccol = 0 if not sample else 2048
                    T.dma("sp", ub[:, 16:16 + nreal], zf[(25 + gi) * 128:(26 + gi) * 128, ccol:ccol + nreal], w=[ub])
                    ms("pool", ub[:, 0:16], 0.0, [ub])
                    if sample:
                        T.dma("sp", ub[:, 1:16], I["st_pool"][l, :, gi * 128:(gi + 1) * 128].rearrange("r c -> c r"), w=[ub])
                    T.dma("sp", gt[:, 0:nreal], zf[(29 + gi) * 128:(30 + gi) * 128, ccol:ccol + nreal], w=[gt])
                    s_c = ub
                    for k_ in range(gi + 1):
                        stp = 2 ** k_
                        s_n = sA if k_ % 2 == 0 else sB
                        tt("dve" if k_ % 2 == 0 else "pool", s_n[:, stp:Wn], s_c[:, stp:Wn], s_c[:, 0:Wn - stp], ALU.add,
                           [s_c], [s_n])
                        cp("pool", s_n[:, 0:stp], s_c[:, 0:stp], [s_c], [s_n])
                        s_c = s_n
                    s_o = sA if s_c is sB else sB
                    ts("dve", s_o[:, 16:Wn], s_c[:, 16:Wn], 1.0 / wsz, ALU.mult, [s_c], [s_o])
                    if not sample:
                        tt("dve", s_o[:, 16:32], s_o[:, 16:32], fixw[:, gi, :], ALU.mult, [s_o, fixw], [s_o])
                    tt("dve", db[:, 0:nreal], s_o[:, 16:Wn], ub[:, 16:Wn], ALU.subtract, [s_o, ub], [db])
                    act(gt[:, 0:nreal], gt[:, 0:nreal], AF.Silu, [gt], [gt])
                    ts("dve", gt[:, 0:nreal], gt[:, 0:nreal], psc[:, gi:gi + 1], ALU.mult, [gt, psc], [gt])
                    nblk = 4 if not sample else 1
                    for tb in range(nblk):
                        n = 512 if not sample else 1
                        bk = nb()
                        mm(bk[:, 0:n], pw[:, gi, :], db[:, tb * 512:tb * 512 + n], True, True, [pw, db], [bk])
                        toks = btok[tb * 4:tb * 4 + 4] if not sample else [btok[16]]
                        tt("dve", big[:, 12 + gi, ccol + tb * 512:ccol + tb * 512 + n], bk[:, 0:n], gt[:, tb * 512:tb * 512 + n],
                           ALU.mult, [bk, gt], toks)
            ms("pool", big[:, 12:16, 2049:NT], 0.0, [btok[16]])
        T.pop()

        if "p5" in phases:
            fuse = FUSE_NORM and l < DEPTH - 1
            xy = [T.sb([128, D], F32, "xy") for _ in range(3)]
            if fuse:
                normw2 = T.sb([128, 16], F32, "normw2")
                T.dma("sp", normw2[:], I["norm_w"][l + 1].rearrange("(k p) -> p k", p=128), w=[normw2])
                junk2 = T.sb([128, D], BF16, "junk2")
                ssq2 = T.sb([128, 2], F32, "ssq2")
            for gc in range(NCH):
                x_ = xy[gc % 3]
                src, rows = x_src(l, gc)
                rd = [xtok[(l - 1) % 2][gc]] if l > 0 else []
                if rows == 1:
                    ms("pool", x_[:], 0.0, [x_])
                T.dma("sp", x_[0:rows, :], src, r=rd, w=[x_])
                for cb in range(4):
                    w_ = wo[cb]
                    bk = nb()
                    for kt in range(16):
                        mm(bk[:], big[:, kt, gc * 128:(gc + 1) * 128], w_[:, kt, :], kt == 0, kt == 15, [w_, btok[gc]], [bk])
                    tt("dve", x_[0:rows, cb * 512:(cb + 1) * 512], bk[0:rows, :], x_[0:rows, cb * 512:(cb + 1) * 512], ALU.add,
                       [bk, x_], [x_])
                dst, rows = y_dst(l, gc)
                T.dma("sp", dst, x_[0:rows, :], r=[x_], w=[xtok[l % 2][gc]])
                if fuse and gc > 0:
                    norm_chunk(xy[(gc - 1) % 3], gc - 1, normw2, junk2, ssq2)
            if fuse:
                norm_chunk(xy[(NCH - 1) % 3], NCH - 1, normw2, junk2, ssq2)
        T.pop()

    T.finish()
    cm2.__exit__(None, None, None)
    cm.__exit__(None, None, None)
    return nc


_NC_CACHE = {}


def make_in_maps(inputs):
    f = lambda a: np.ascontiguousarray(np.asarray(a, dtype=np.float32))
    g = {k: f(v) for k, v in inputs.items()}
    maps = []
    for c in range(8):
        b = c % 4
        m = {
            "x_p": g["x_prompt"][b], "x_s": g["x_sample"][c],
            "st_wkv": f(g["state_wkv"][:, c]), "st_shift": f(g["state_shift"][:, c]),
            "st_pool": f(g["state_pool"][:, c]),
            "ck0": f(g["cache_k_w128"][:, c].reshape(4, 128, 256)), "cv0": f(g["cache_v_w128"][:, c].reshape(4, 128, 256)),
            "ck1": f(g["cache_k_w512"][:, c].reshape(4, 512, 256)), "cv1": f(g["cache_v_w512"][:, c].reshape(4, 512, 256)),
            "ck2": f(g["cache_k_w2048"][:, c].reshape(4, 2048, 256)), "cv2": f(g["cache_v_w2048"][:, c].reshape(4, 2048, 256)),
            "norm_w": g["norm_w"], "w_in": g["w_in"], "w_out": g["w_out"], "rwkv_mu": g["rwkv_mu"],
            "rwkv_w0": g["rwkv_w0"], "rwkv_w_up": g["rwkv_w_up"], "rwkv_a0": g["rwkv_a0"], "rwkv_a_up": g["rwkv_a_up"],
            "rwkv_k_k": g["rwkv_k_k"], "rwkv_k_a": g["rwkv_k_a"], "rwkv_r_k": f(g["rwkv_r_k"].reshape(4, 768)),
            "rwkv_ln_w": g["rwkv_ln_w"], "rwkv_ln_b": g["rwkv_ln_b"], "q_norm_w": g["q_norm_w"],
            "k_norm_w": g["k_norm_w"], "pool_w": g["pool_w"], "pool_scale": g["pool_scale"],
        }
        maps.append(m)
    return maps


def assemble(res):
    R = res
    st = lambda n, cores: np.stack([R[c][n] for c in cores], axis=1)
    pc = [0, 1, 2, 3]
    sc = list(range(8))
    out = [
        np.stack([R[c]["y_p"] for c in pc], axis=0),
        np.stack([R[c]["y_s"] for c in sc], axis=0),
        st("p_wkv", pc), st("p_shift", pc), st("p_pool", pc),
    ]
    for g, keep in enumerate((128, 512, 2048)):
        out.append(st("p_k%d" % g, pc).reshape(4, 4, keep, 4, 64))
        out.append(st("p_v%d" % g, pc).reshape(4, 4, keep, 4, 64))
    out += [st("s_wkv", sc), st("s_shift", sc), st("s_pool", sc)]
    for g in range(3):
        out.append(st("s_k%d" % g, sc).reshape(4, 8, 1, 4, 64))
        out.append(st("s_v%d" % g, sc).reshape(4, 8, 1, 4, 64))
    return tuple(np.ascontiguousarray(o, dtype=np.float32) for o in out)


def kernel(**inputs):
    nc = build()
    maps = make_in_maps(inputs)
    res = run_bass_kernel_spmd(nc, maps, core_ids=list(range(8)))
    return assemble(res.results)
```

```python
import contextlib
import math
import numpy as np
import concourse.bass as bass
import concourse.mybir as mybir
from concourse.bass_utils import run_bass_kernel_spmd

F32 = mybir.dt.float32
BF16 = mybir.dt.bfloat16
I32 = mybir.dt.int32
AF = mybir.ActivationFunctionType
ALU = mybir.AluOpType
AX = mybir.AxisListType

D = 2048
S = 2048
NT = 2176
NCH = 17
DIN = 7296
NORM_EPS = 1e-6
GN_EPS = 64e-5
C0 = math.exp(-0.5)
ROPE_THETA = 500000.0
PAST = 16384.0
PREP_W = 4
LORA_DELAY = 3
MIX_DELAY = 2
BON_DELAY = 0
FUSE_NORM = True
SIM_SW_SEMS = False


class Buf:
    __slots__ = ("t", "lw", "rd", "name")

    def __init__(self, t=None, name=""):
        self.t = t
        self.lw = None
        self.rd = {}
        self.name = name

    def __getitem__(self, k):
        return self.t[k]


class Trk:
    def __init__(self, nc, n_dma_sems=32):
        self.nc = nc
        self.es = contextlib.ExitStack()
        self.eng = {"pe": nc.tensor, "act": nc.scalar, "dve": nc.vector, "pool": nc.gpsimd, "sp": nc.sync}
        self.sem = {}
        self.cnt = {}
        for e in self.eng:
            self.sem[e] = self.es.enter_context(nc.semaphore("sem_" + e))
            self.cnt[e] = 0
        self.dsem = [self.es.enter_context(nc.semaphore("dsem%d" % i)) for i in range(n_dma_sems)]
        self.dcnt = [0] * n_dma_sems
        self.dnext = 0
        self.waited = {e: {} for e in self.eng}
        self.nbuf = 0
        self.ninst = 0
        self.scopes = []
        self.psem = {}

    def push(self):
        st = contextlib.ExitStack()
        self.scopes.append(st)

    def pop(self):
        self.barrier()
        self.scopes.pop().close()

    def _stack(self):
        return self.scopes[-1] if self.scopes else self.es

    def sb(self, shape, dt, name=None):
        self.nbuf += 1
        name = (name or "sb") + "_%d" % self.nbuf
        t = self._stack().enter_context(self.nc.sbuf_tensor(name, list(shape), dt))
        return Buf(t, name)

    def ps(self, shape, dt, name=None):
        self.nbuf += 1
        name = (name or "ps") + "_%d" % self.nbuf
        t = self.es.enter_context(self.nc.psum_tensor(name, list(shape), dt))
        return Buf(t, name)

    def tok(self, name=""):
        return Buf(None, name)

    def _need(self, e, deps):
        eng = self.eng[e]
        w = self.waited[e]
        best = {}
        for (s, v) in deps:
            k = id(s)
            if v > w.get(k, 0) and v > best.get(k, (None, 0))[1]:
                best[k] = (s, v)
        for k, (s, v) in best.items():
            eng.wait_ge(s, v)
            w[k] = v
            self.ninst += 1

    def _deps(self, e, r, w):
        deps = []
        for b in r:
            if b.lw is not None:
                deps.append(b.lw)
        for b in w:
            if b.lw is not None:
                deps.append(b.lw)
            deps.extend(b.rd.values())
        if e == "pe":
            deps = [d for d in deps if d[0] is not self.sem["pe"]]
        return deps

    def _mark(self, sig, r, w):
        k = id(sig[0])
        for b in r:
            if b.rd.get(k, (None, 0))[1] < sig[1]:
                b.rd[k] = sig
        for b in w:
            b.lw = sig
            b.rd = {}

    def op(self, e, fn, r=(), w=()):
        self._need(e, self._deps(e, r, w))
        ins = fn(self.eng[e])
        self.cnt[e] += 1
        ins.then_inc(self.sem[e], 1)
        self._mark((self.sem[e], self.cnt[e]), r, w)
        self.ninst += 1
        return ins

    def dma(self, q, out, in_, r=(), w=(), **kw):
        if q == "pool" and SIM_SW_SEMS:
            return self.dma_sw(out, in_, r, w, **kw)
        i = self.dnext
        self.dnext = (self.dnext + 1) % len(self.dsem)
        s = self.dsem[i]
        deps = self._deps(q, r, w)
        if self.dcnt[i] > 0:
            deps.append((s, 16 * self.dcnt[i]))
        self._need(q, deps)
        ins = self.eng[q].dma_start(out=out, in_=in_, **kw)
        self.dcnt[i] += 1
        ins.then_inc(s, 16)
        self._mark((s, 16 * self.dcnt[i]), r, w)
        self.ninst += 1
        return ins

    def dma_sw(self, out, in_, r=(), w=(), **kw):
        b = w[0]
        key = id(b)
        if key not in self.psem:
            self.psem[key] = self.es.enter_context(self.nc.semaphore("psem%d" % len(self.psem)))
        s = self.psem[key]
        self._need("pool", self._deps("pool", r, w))
        self.eng["pool"].sem_clear(s)
        for e in self.waited:
            self.waited[e].pop(id(s), None)
        ins = self.eng["pool"].dma_start(out=out, in_=in_, **kw)
        ins.then_inc(s, 16)
        self._mark((s, 16), r, w)
        self.ninst += 2
        return ins

    def barrier(self):
        sigs = [(self.sem[e], self.cnt[e]) for e in self.eng if self.cnt[e] > 0]
        sigs += [(self.dsem[i], 16 * self.dcnt[i]) for i in range(len(self.dsem)) if self.dcnt[i] > 0]
        for e in self.eng:
            self._need(e, [s for s in sigs if s[0] is not self.sem[e]])

    def finish(self):
        self.barrier()
        while self.scopes:
            self.scopes.pop().close()
        self.es.close()


IN_SPECS = [
    ("x_p", [S, D]), ("x_s", [1, D]), ("st_wkv", [4, 12, 64, 64]), ("st_shift", [4, 3200]),
    ("st_pool", [4, 15, 512]),
    ("ck0", [4, 128, 256]), ("cv0", [4, 128, 256]), ("ck1", [4, 512, 256]), ("cv1", [4, 512, 256]),
    ("ck2", [4, 2048, 256]), ("cv2", [4, 2048, 256]),
    ("norm_w", [4, D]), ("w_in", [4, D, DIN]), ("w_out", [4, D, D]), ("rwkv_mu", [4, 3200]),
    ("rwkv_w0", [4, 768]), ("rwkv_w_up", [4, 64, 768]), ("rwkv_a0", [4, 768]), ("rwkv_a_up", [4, 64, 768]),
    ("rwkv_k_k", [4, 768]), ("rwkv_k_a", [4, 768]), ("rwkv_r_k", [4, 768]), ("rwkv_ln_w", [4, 768]),
    ("rwkv_ln_b", [4, 768]), ("q_norm_w", [4, 64]), ("k_norm_w", [4, 64]), ("pool_w", [4, 4, 128, 128]),
    ("pool_scale", [4, 512]),
]
OUT_SPECS = [
    ("y_p", [S, D]), ("y_s", [1, D]),
    ("p_wkv", [4, 12, 64, 64]), ("p_shift", [4, 3200]), ("p_pool", [4, 15, 512]),
    ("p_k0", [4, 128, 256]), ("p_v0", [4, 128, 256]), ("p_k1", [4, 512, 256]), ("p_v1", [4, 512, 256]),
    ("p_k2", [4, 2048, 256]), ("p_v2", [4, 2048, 256]),
    ("s_wkv", [4, 12, 64, 64]), ("s_shift", [4, 3200]), ("s_pool", [4, 15, 512]),
    ("s_k0", [4, 1, 256]), ("s_v0", [4, 1, 256]), ("s_k1", [4, 1, 256]), ("s_v1", [4, 1, 256]),
    ("s_k2", [4, 1, 256]), ("s_v2", [4, 1, 256]),
]
DILS = (1, 4, 16)


def blk_cols(g, bi):
    if bi == 16:
        return 2048, 1
    if g == 0:
        return bi * 128, 1
    if g == 1:
        return (bi // 4) * 512 + (bi % 4), 4
    return bi, 16


def blk_prev(g, bi):
    if bi == 16:
        return "cache"
    if g == 0:
        return bi - 1 if bi >= 1 else None
    if g == 1:
        return bi - 4 if bi >= 4 else None
    return None


def build(DEPTH=4, phases=("p0", "p1", "p2", "p3", "p4", "p5"), lim=None):
    PCH = list(range(16)) if lim is None else list(lim["chunks"])
    CHS = PCH + [16]
    FMT = list(range(33)) if lim is None else list(lim["tiles"])
    TMK = ("vg", "av", "ag", "pu") if lim is None else tuple(lim["kinds"])
    TBS = sorted(set(c // 4 for c in PCH)) + [4]
    STG = 99 if lim is None else lim.get("stage", 99)
    SUB = "" if lim is None else lim.get("sub", "")
    nc = bass.Bass("TRN2", target_bir_lowering=False)
    I = {n: nc.dram_tensor(n, s, F32, kind="ExternalInput").ap() for n, s in IN_SPECS}
    O = {n: nc.dram_tensor(n, s, F32, kind="ExternalOutput").ap() for n, s in OUT_SPECS}
    scr = lambda n, s: nc.dram_tensor(n, s, F32, kind="Internal").ap()
    zf = scr("zf", [33 * 128, NT])
    zt_vg = scr("zt_vg", [NT, 1536])
    zt_av = scr("zt_av", [3, NT, 256])
    zt_ag = scr("zt_ag", [NT, 768])
    oacc = scr("oacc", [3, NT, 260])
    xbuf = [scr("xbuf0", [S, D]), scr("xbuf1", [S, D])]
    xsb = [scr("xsb0", [1, D]), scr("xsb1", [1, D])]

    T = Trk(nc)
    cm = nc.allow_non_contiguous_dma(reason="small strided parameter loads")
    cm.__enter__()
    cm2 = nc.allow_low_precision(reason="bf16 matmul operands, fp32 accumulate")
    cm2.__enter__()
    NC = True

    def tt(e, out, a, b, op, r, w):
        return T.op(e, lambda E: E.tensor_tensor(out=out, in0=a, in1=b, op=op), r=r, w=w)

    def ts(e, out, a, s1, op0, r, w, s2=None, op1=None):
        if op1 is None:
            return T.op(e, lambda E: E.tensor_scalar(out=out, in0=a, scalar1=s1, scalar2=None, op0=op0), r=r, w=w)
        return T.op(e, lambda E: E.tensor_scalar(out=out, in0=a, scalar1=s1, scalar2=s2, op0=op0, op1=op1), r=r, w=w)

    def stt(e, out, a, s, b, op0, op1, r, w):
        return T.op(e, lambda E: E.scalar_tensor_tensor(out=out, in0=a, scalar=s, in1=b, op0=op0, op1=op1), r=r, w=w)

    def act(out, in_, func, r, w, scale=1.0, bias=None, accum=None):
        kw = {}
        if bias is not None:
            kw["bias"] = bias
        if accum is not None:
            kw["accum_out"] = accum
        return T.op("act", lambda E: E.activation(out=out, in_=in_, func=func, scale=scale, **kw), r=r, w=w)

    def cp(e, out, in_, r, w):
        if e == "act":
            return T.op("act", lambda E: E.copy(out=out, in_=in_), r=r, w=w)
        return T.op(e, lambda E: E.tensor_copy(out=out, in_=in_), r=r, w=w)

    def mm(out, lhsT, rhs, st, sp, r, w):
        return T.op("pe", lambda E: E.matmul(out, lhsT=lhsT, rhs=rhs, start=st, stop=sp), r=r, w=w)

    def tp(out, in_, ident, r, w):
        return T.op("pe", lambda E: E.transpose(out=out, in_=in_, identity=ident), r=r, w=w)

    def ms(e, ap, val, w):
        return T.op(e, lambda E: E.memset(ap, val), w=w)

    ev_i = [0]

    def evac(out, in_, r, w):
        ev_i[0] += 1
        return cp("act" if ev_i[0] % 2 else "dve", out, in_, r, w)

    def bc(ap, shape, axis):
        return ap.unsqueeze(axis).to_broadcast(list(shape))

    pb = [T.ps([128, 512], F32, "bank") for _ in range(8)]
    pbi = [0]

    def nb():
        pbi[0] = (pbi[0] + 1) % 8
        return pb[pbi[0]]

    big = T.sb([128, 16, NT], BF16, "big")
    btok = [T.tok("big%d" % c) for c in range(NCH)]
    ones = T.sb([128, 128], F32, "ones")
    identF = T.sb([128, 128], F32, "identF")
    identB = T.sb([128, 128], BF16, "identB")
    m_su = T.sb([128, 4, 128], BF16, "m_su")
    m_sl = T.sb([128, 4, 128], BF16, "m_sl")
    m_u = T.sb([128, 4, 128], BF16, "m_u")
    amask = T.sb([128, 2, 2, 128], BF16, "amask")
    bones = T.sb([128, 128], F32, "bones")
    bsel = T.sb([128, 2], BF16, "bsel")
    prot = T.sb([128, 128], F32, "prot")
    fixw = T.sb([128, 4, 16], F32, "fixw")
    ropetab = scr("ropetab", [2, 128, NT])

    T.push()
    cosT = T.sb([128, NT], F32, "cosT")
    sinT = T.sb([128, NT], F32, "sinT")
    ms("pool", ones[:], 1.0, [ones])
    asel = lambda out, pat, op, base, cmul, w: T.op(
        "pool", lambda E: E.affine_select(out=out, in_=ones_src(out), pattern=pat, compare_op=op, fill=0.0,
                                          base=base, channel_multiplier=cmul), r=[ones], w=w)

    def ones_src(out):
        shp = out.shape
        if len(shp) == 2:
            return ones[:, 0:shp[1]]
        return bc(ones[:, 0:shp[-1]], shp, 1)

    T.op("pool", lambda E: E.affine_select(out=identF[:], in_=ones[:], pattern=[[-1, 128]], compare_op=ALU.is_equal,
                                           fill=0.0, base=0, channel_multiplier=1), r=[ones], w=[identF])
    cp("dve", identB[:], identF[:], [identF], [identB])
    ones4 = T.sb([128, 4, 128], F32, "ones4")
    ms("pool", ones4[:], 1.0, [ones4])
    T.op("pool", lambda E: E.affine_select(out=m_su[:], in_=ones4[:], pattern=[[0, 4], [1, 128]], compare_op=ALU.is_ge,
                                           fill=0.0, base=-1, channel_multiplier=-1), r=[ones4], w=[m_su])
    T.op("pool", lambda E: E.affine_select(out=m_sl[:], in_=ones4[:], pattern=[[0, 4], [-1, 128]], compare_op=ALU.is_ge,
                                           fill=0.0, base=-1, channel_multiplier=1), r=[ones4], w=[m_sl])
    T.op("pool", lambda E: E.affine_select(out=m_u[:], in_=ones4[:], pattern=[[0, 4], [1, 128]], compare_op=ALU.is_ge,
                                           fill=0.0, base=0, channel_multiplier=-1), r=[ones4], w=[m_u])
    amf = T.sb([128, 2, 2, 128], F32, "amf")
    for h2 in range(2):
        T.op("pool", lambda E, h2=h2: E.affine_select(out=amf[:, h2, 0, :], in_=ones[:], pattern=[[-1, 128]],
                                                      compare_op=ALU.is_ge, fill=0.0, base=0, channel_multiplier=1),
             r=[ones], w=[amf])
        T.op("pool", lambda E, h2=h2: E.affine_select(out=amf[:, h2, 1, :], in_=ones[:], pattern=[[1, 128]],
                                                      compare_op=ALU.is_ge, fill=0.0, base=0, channel_multiplier=-1),
             r=[ones], w=[amf])
    cp("dve", amask[:], amf[:], [amf], [amask])
    ms("pool", bones[:], 0.0, [bones])
    ms("pool", bones[0:64, 0:64], 1.0, [bones])
    ms("pool", bones[64:128, 64:128], 1.0, [bones])
    ms("pool", bsel[:], 0.0, [bsel])
    ms("pool", bsel[0:64, 0:1], 1.0, [bsel])
    ms("pool", bsel[64:128, 1:2], 1.0, [bsel])
    t1 = T.sb([128, 128], F32, "t1")
    t2 = T.sb([128, 128], F32, "t2")
    T.op("pool", lambda E: E.affine_select(out=t1[:], in_=ones[:], pattern=[[-1, 128]], compare_op=ALU.is_equal,
                                           fill=0.0, base=-8, channel_multiplier=1), r=[ones], w=[t1])
    T.op("pool", lambda E: E.affine_select(out=t2[:], in_=ones[:], pattern=[[-1, 128]], compare_op=ALU.is_equal,
                                           fill=0.0, base=8, channel_multiplier=1), r=[ones], w=[t2])
    for base in (0, 64):
        ms("pool", t1[:, base + 8:base + 64], 0.0, [t1])
        ms("pool", t2[:, base:base + 8], 0.0, [t2])
        ms("pool", t2[:, base + 16:base + 64], 0.0, [t2])
    tt("pool", prot[:], t2[:], t1[:], ALU.subtract, [t1, t2], [prot])
    ms("pool", fixw[:], 1.0, [fixw])
    for gi in range(4):
        w_ = 2 ** (gi + 1)
        for t_ in range(w_ - 1):
            ms("pool", fixw[:, gi, t_:t_ + 1], float(w_) / float(t_ + 1), [fixw])
    invrow = T.sb([1, 128], F32, "invrow")
    posrow = T.sb([1, NT], F32, "posrow")
    posi = T.sb([1, NT], I32, "posi")
    ms("pool", invrow[:], 0.0, [invrow])
    for base in (0, 64):
        for e_ in range(16):
            val = float(np.float32(ROPE_THETA) ** np.float32(-(e_ % 8) * 2.0 / 16.0))
            ms("pool", invrow[0:1, base + e_:base + e_ + 1], val, [invrow])
    T.op("pool", lambda E: E.iota(posi[:], pattern=[[1, NT]], base=0, channel_multiplier=0), w=[posi])
    cp("dve", posrow[:], posi[:], [posi], [posrow])
    ms("dve", posrow[0:1, 2048:NT], PAST, [posrow])
    yy = T.sb([128, 512], F32, "yy")
    yi = T.sb([128, 512], I32, "yi")
    yf = T.sb([128, 512], F32, "yf")
    for tb in range(5):
        n = 512 if tb < 4 else 128
        cs_ = slice(tb * 512, tb * 512 + n)
        bk = nb()
        mm(bk[:, 0:n], invrow[0:1, :], posrow[0:1, cs_], True, True, [invrow, posrow], [bk])
        for (tab, off) in ((sinT, 0.5), (cosT, 0.75)):
            ts("dve", yy[:, 0:n], bk[:, 0:n], 1.0 / (2 * math.pi), ALU.mult, [bk], [yy], s2=off, op1=ALU.add)
            cp("dve", yi[:, 0:n], yy[:, 0:n], [yy], [yi])
            cp("dve", yf[:, 0:n], yi[:, 0:n], [yi], [yf])
            tt("dve", yy[:, 0:n], yy[:, 0:n], yf[:, 0:n], ALU.subtract, [yy, yf], [yy])
            ts("dve", yf[:, 0:n], yy[:, 0:n], 0.0, ALU.is_lt, [yy], [yf])
            tt("dve", yy[:, 0:n], yy[:, 0:n], yf[:, 0:n], ALU.add, [yy, yf], [yy])
            ts("dve", yy[:, 0:n], yy[:, 0:n], 0.0, ALU.max, [yy], [yy], s2=0.9999999, op1=ALU.min)
            ts("dve", yy[:, 0:n], yy[:, 0:n], 2 * math.pi, ALU.mult, [yy], [yy], s2=-math.pi, op1=ALU.add)
            ts("dve", yy[:, 0:n], yy[:, 0:n], -3.1415925, ALU.max, [yy], [yy], s2=3.1415925, op1=ALU.min)
            act(tab[:, cs_], yy[:, 0:n], AF.Sin, [yy], [tab])

    T.dma("sp", ropetab[0], cosT[:], r=[cosT])
    T.dma("sp", ropetab[1], sinT[:], r=[sinT])
    T.pop()
    eps_n = T.sb([128, 1], F32, "eps_n")
    ms("pool", eps_n[:], NORM_EPS, [eps_n])
    rowm0 = T.sb([128, 1], F32, "rowm0")
    ms("pool", rowm0[:], 0.0, [rowm0])
    ms("pool", rowm0[0:1, :], 1.0, [rowm0])
    eps_g = T.sb([128, 1], F32, "eps_g")
    ms("pool", eps_g[:], GN_EPS, [eps_g])

    xtok = [[T.tok("x%d_%d" % (b, c)) for c in range(NCH)] for b in range(2)]

    def x_src(l, c):
        if c < 16:
            src = I["x_p"] if l == 0 else xbuf[(l - 1) % 2]
            return src[c * 128:(c + 1) * 128, :], 128
        src = I["x_s"] if l == 0 else xsb[(l - 1) % 2]
        return src[0:1, :], 1

    def y_dst(l, c):
        last = l == DEPTH - 1
        if c < 16:
            dst = O["y_p"] if last else xbuf[l % 2]
            return dst[c * 128:(c + 1) * 128, :], 128
        dst = O["y_s"] if last else xsb[l % 2]
        return dst[0:1, :], 1

    def fm_col0(ti):
        if ti < 12:
            return ti * 128
        if ti == 12:
            return 3072
        if ti < 25:
            return 3200 + (ti - 13) * 128
        return 6272 + (ti - 25) * 128

    def norm_chunk(xt, c, normw, junk, ssq):
        act(junk[:], xt[:], AF.Square, [xt], [junk, ssq], accum=ssq[:, 0:1])
        act(ssq[:, 1:2], ssq[:, 0:1], AF.Sqrt, [ssq, eps_n], [ssq], scale=1.0 / D, bias=eps_n[:, 0:1])
        T.op("dve", lambda E: E.reciprocal(out=ssq[:, 1:2], in_=ssq[:, 1:2]), r=[ssq], w=[ssq])
        ts("dve", xt[:], xt[:], ssq[:, 1:2], ALU.mult, [xt, ssq], [xt])
        for q in range(4):
            bk = nb()
            for k4 in range(4):
                kt = q * 4 + k4
                tp(bk[:, k4 * 128:(k4 + 1) * 128], xt[:, kt * 128:(kt + 1) * 128], identF[:], [xt, identF], [bk])
            tt("dve", big[:, q * 4:(q + 1) * 4, c * 128:(c + 1) * 128],
               bk[:].rearrange("p (a b) -> p a b", a=4), bc(normw[:, q * 4:(q + 1) * 4], [128, 4, 128], 2),
               ALU.mult, [bk, normw], [btok[c]])

    for l in range(DEPTH):
        T.push()
        if l == 0 or not FUSE_NORM:
            normw = T.sb([128, 16], F32, "normw")
            T.dma("sp", normw[:], I["norm_w"][l].rearrange("(k p) -> p k", p=128), w=[normw])
            xin = [T.sb([128, D], F32, "xin") for _ in range(2)]
            junk = T.sb([128, D], BF16, "junk")
            ssq = T.sb([128, 2], F32, "ssq")
            for c in CHS:
                xt = xin[c % 2]
                src, rows = x_src(l, c)
                rd = [xtok[(l - 1) % 2][c]] if l > 0 else []
                if rows == 1:
                    ms("pool", xt[:], 0.0, [xt])
                T.dma("sp", xt[0:rows, :], src, r=rd, w=[xt])
                norm_chunk(xt, c, normw, junk, ssq)
        T.pop()

        T.push()
        wf = [T.sb([128, 16, 128], BF16, "wf") for _ in range(2)]
        stg = [T.sb([128, 512], F32, "stg") for _ in range(4)]
        sti = [0]

        def nstg():
            sti[0] = (sti[0] + 1) % 4
            return stg[sti[0]]

        for ti in FMT:
            w_ = wf[ti % 2]
            c0 = fm_col0(ti)
            T.dma("pool", w_[:], I["w_in"][l][:, c0:c0 + 128].rearrange("(k p) c -> p k c", p=128), w=[w_])
            for tb in TBS:
                n = 512 if tb < 4 else 128
                bk = nb()
                for kt in range(16):
                    mm(bk[:, 0:n], w_[:, kt, :], big[:, kt, tb * 512:tb * 512 + n], kt == 0, kt == 15,
                       [w_] + btok[tb * 4:tb * 4 + (4 if tb < 4 else 1)], [bk])
                st = nstg()
                evac(st[:, 0:n], bk[:, 0:n], [bk], [st])
                T.dma("sp", zf[ti * 128:(ti + 1) * 128, tb * 512:tb * 512 + n], st[:, 0:n], r=[st])
                if ti <= 12 and tb == 3:
                    T.dma("sp", O["p_shift"][l, c0:c0 + 128].rearrange("(p o) -> p o", o=1), st[:, 511:512], r=[st])
                if ti <= 12 and tb == 4:
                    T.dma("sp", O["s_shift"][l, c0:c0 + 128].rearrange("(p o) -> p o", o=1), st[:, 0:1], r=[st])
        wt = [T.sb([128, 16, 512], BF16, "wt") for _ in range(2)]
        tmb = [(1536, 512, "vg", 0), (2048, 512, "vg", 512), (2560, 512, "vg", 1024),
               (4736, 256, "av", 0), (4992, 256, "av", 1), (5248, 256, "av", 2),
               (5504, 512, "ag", 0), (6016, 256, "ag", 512), (6272, 512, "pu", 0)]
        for bi_, (c0, ncol, kind, arg) in enumerate(tmb):
            if kind not in TMK:
                continue
            w_ = wt[bi_ % 2]
            T.dma("pool", w_[:, :, 0:ncol], I["w_in"][l][:, c0:c0 + ncol].rearrange("(k p) c -> p k c", p=128), w=[w_])
            for c in CHS:
                if kind == "pu" and c < 15:
                    continue
                if kind == "av":
                    start, step = blk_cols(arg, c)
                else:
                    start, step = c * 128, 1
                if step == 1:
                    toks = [btok[start // 128]]
                elif step == 4:
                    toks = btok[(start // 512) * 4:(start // 512) * 4 + 4]
                else:
                    toks = btok[0:16]
                bk = nb()
                for kt in range(16):
                    lhs = big[:, kt, start:start + 128] if step == 1 else big[:, kt, bass.ds(start, 128, step=step)]
                    mm(bk[:, 0:ncol], lhs, w_[:, kt, 0:ncol], kt == 0, kt == 15, [w_] + toks, [bk])
                st = nstg()
                evac(st[:, 0:ncol], bk[:, 0:ncol], [bk], [st])
                rows = slice(c * 128, (c + 1) * 128)
                if kind == "vg":
                    T.dma("sp", zt_vg[rows, arg:arg + ncol], st[:, 0:ncol], r=[st])
                    if c == 15:
                        T.dma("sp", O["p_shift"][l:l + 1, c0:c0 + ncol], st[127:128, 0:ncol], r=[st])
                    if c == 16:
                        T.dma("sp", O["s_shift"][l:l + 1, c0:c0 + ncol], st[0:1, 0:ncol], r=[st])
                elif kind == "ag":
                    T.dma("sp", zt_ag[rows, arg:arg + ncol], st[:, 0:ncol], r=[st])
                elif kind == "av":
                    g = arg
                    T.dma("sp", zt_av[g, rows, :], st[:, 0:256], r=[st])
                    if c == 16:
                        T.dma("sp", O["s_v%d" % g][l, 0:1, :], st[0:1, 0:256], r=[st])
                    elif g == 0 and c == 15:
                        T.dma("sp", O["p_v0"][l, :, :], st[:, 0:256], r=[st])
                    elif g == 1 and c >= 12:
                        dst = O["p_v1"][l].rearrange("(i d) c -> i d c", d=4)[:, c % 4, :]
                        T.dma("sp", dst, st[:, 0:256], r=[st])
                    elif g == 2:
                        dst = O["p_v2"][l].rearrange("(i d) c -> i d c", d=16)[:, c, :]
                        T.dma("sp", dst, st[:, 0:256], r=[st])
                elif kind == "pu":
                    if c == 15:
                        T.dma("sp", O["p_pool"][l, :, :], st[113:128, 0:512], r=[st])
                    else:
                        T.dma("sp", O["s_pool"][l, 14:15, :], st[0:1, 0:512], r=[st])
                        T.dma("sp", O["s_pool"][l, 0:14, :], I["st_pool"][l, 1:15, :])
        T.pop()

        T.push()
        mu_fm = T.sb([128, 13], F32, "mu_fm")
        T.dma("sp", mu_fm[:, 0:12], I["rwkv_mu"][l, 0:1536].rearrange("(t p) -> p t", p=128), w=[mu_fm])
        T.dma("sp", mu_fm[:, 12:13], I["rwkv_mu"][l, 3072:3200].rearrange("(p o) -> p o", o=1), w=[mu_fm])
        mu_tm = T.sb([128, 1536], F32, "mu_tm")
        T.dma("sp", mu_tm[:], I["rwkv_mu"][l, 1536:3072].partition_broadcast(128), w=[mu_tm])
        pr = T.sb([128, 5, 6], F32, "pr")
        for i_, nm in enumerate(("rwkv_w0", "rwkv_a0", "rwkv_k_k", "rwkv_k_a", "rwkv_r_k")):
            T.dma("sp", pr[:, i_, :], I[nm][l].rearrange("(t p) -> p t", p=128), w=[pr])
        omka = T.sb([128, 6], F32, "omka")
        ts("dve", omka[:], pr[:, 3, :], -1.0, ALU.mult, [pr], [omka], s2=1.0, op1=ALU.add)
        lnw = T.sb([128, 768], F32, "lnw")
        lnb = T.sb([128, 768], F32, "lnb")
        T.dma("sp", lnw[:], I["rwkv_ln_w"][l].partition_broadcast(128), w=[lnw])
        T.dma("sp", lnb[:], I["rwkv_ln_b"][l].partition_broadcast(128), w=[lnb])
        lw_ = T.sb([128, 768], BF16, "loraw")
        T.dma("pool", lw_[0:64, :], I["rwkv_w_up"][l], w=[lw_])
        lwa_tok = T.tok("lwa")
        T.dma("pool", lw_[64:128, :], I["rwkv_a_up"][l], w=[lwa_tok])
        Hst = T.sb([128, 6, 64], F32, "Hst")
        Hbf = T.sb([128, 6, 64], BF16, "Hbf")
        zfc = T.sb([128, 13, 129], F32, "zfc")
        zs = T.sb([128, 13, 128], F32, "zs")
        zt = T.sb([128, 1536], F32, "zt")
        ztp = T.sb([128, 1536], F32, "ztp")
        lor = T.sb([128, 128], BF16, "lor")
        sigw = T.sb([128, 6, 128], F32, "sigw")
        aa = T.sb([128, 6, 128], F32, "aa")
        kk = T.sb([128, 6, 128], F32, "kk")
        tA = T.sb([128, 6, 128], F32, "tA")
        tB = T.sb([128, 6, 128], F32, "tB")
        tK1 = T.sb([128, 6, 128], F32, "tK1")
        tK2 = T.sb([128, 6, 128], F32, "tK2")
        cs = T.sb([128, 6, 128], F32, "cs")
        enE = T.sb([128, 6, 128], F32, "enE")
        bhat = T.sb([128, 6, 128], BF16, "bhat")
        khat = T.sb([128, 6, 128], BF16, "khat")
        prodb = T.sb([128, 6, 128], BF16, "prodb")
        Ml = [T.sb([128, 12, 128], BF16, "Ml") for _ in range(2)]
        Xb = [T.sb([128, 768], BF16, "Xb") for _ in range(2)]
        Oo = T.sb([128, 12, 64], F32, "Oo")
        Os = T.sb([128, 12, 64], F32, "Os")
        st12 = T.sb([128, 4, 12], F32, "st12")
        mixa = T.sb([128, 768], BF16, "mixa")
        hout = T.sb([128, 3, 128], F32, "hout")

        class CS:
            pass
        sets = [CS(), CS()]
        s0 = sets[0]
        s0.rhat = T.sb([128, 6, 128], BF16, "rhat")
        s0.ahat = T.sb([128, 6, 128], BF16, "ahat")
        s0.Bt = T.sb([128, 6, 128], BF16, "Bt")
        s0.Kt = T.sb([128, 6, 128], BF16, "Kt")
        s0.vbf = T.sb([128, 768], BF16, "vbf")
        s0.sgb = T.sb([128, 768], BF16, "sgb")
        s0.Nl = [T.sb([128, 12, 128], BF16, "Nl") for _ in range(7)]
        s0.MakT = T.sb([128, 12, 128], BF16, "MakT")
        s0.MrbT = T.sb([128, 12, 128], BF16, "MrbT")
        s0.MrkT = T.sb([128, 12, 128], BF16, "MrkT")
        flat = big[:, 6:16, :].rearrange("p a b -> p (a b)")
        off = [0]

        def carve(shape):
            n = int(np.prod(shape))
            v = flat[:, off[0]:off[0] + n]
            off[0] += n
            if len(shape) == 2:
                v = v.rearrange("p (a b) -> p a b", a=shape[0])
            return Buf(v, "carve")
        s1 = sets[1]
        s1.rhat = carve([6, 128])
        s1.ahat = carve([6, 128])
        s1.Bt = carve([6, 128])
        s1.Kt = carve([6, 128])
        s1.vbf = carve([768])
        s1.sgb = carve([768])
        s1.Nl = [carve([12, 128]) for _ in range(7)]
        s1.MakT = carve([12, 128])
        s1.MrbT = carve([12, 128])
        s1.MrkT = carve([12, 128])
        assert off[0] <= 10 * NT
        for S_ in sets:
            S_.wc = T.sb([128, 6], F32, "wc")
            S_.bon = T.sb([128, 12], F32, "bon")

        def sl_info(sl):
            j, par = sl % 6, sl // 6
            return j, par, slice(par * 64, par * 64 + 64), 2 * j + par

        utoks = {}

        def U(buf, slots):
            k = id(buf)
            if k not in utoks:
                utoks[k] = [T.tok("u") for _ in range(6)]
            return [utoks[k][u] for u in sorted(set(sl // 2 for sl in slots))]

        def XH(buf, half):
            k = (id(buf), "x")
            if k not in utoks:
                utoks[k] = [T.tok("xh") for _ in range(2)]
            return utoks[k][half]

        pbp = [0]
        pbs = [0]

        def nbp():
            pbp[0] = (pbp[0] + 1) % 4
            return pb[pbp[0]]

        def nbs():
            pbs[0] = (pbs[0] + 1) % 4
            return pb[4 + pbs[0]]

        def gen_prep(gc, first, sample, S_):
            col0 = gc * 128
            zsrc = zf[0:13 * 128, :].rearrange("(t p) n -> p t n", p=128)
            if first:
                T.dma("sp", zfc[:, :, 1:129], zsrc[:, :, col0:col0 + 128], w=[zfc])
                if sample:
                    T.dma("sp", zfc[:, 0:12, 0:1], I["st_shift"][l, 0:1536].rearrange("(t p o) -> p t o", p=128, o=1), w=[zfc])
                    T.dma("sp", zfc[:, 12, 0:1], I["st_shift"][l, 3072:3200].rearrange("(p o) -> p o", o=1), w=[zfc])
                else:
                    ms("pool", zfc[:, :, 0:1], 0.0, [zfc])
            else:
                T.dma("sp", zfc[:], zsrc[:, :, col0 - 1:col0 + 128], w=[zfc])
            T.dma("sp", zt[:], zt_vg[col0:col0 + 128, :], w=[zt])
            if first:
                T.dma("sp", ztp[1:128, :], zt_vg[col0:col0 + 127, :], w=[ztp])
                if sample:
                    T.dma("sp", ztp[0:1, :], I["st_shift"][l:l + 1, 1536:3072], w=[ztp])
                else:
                    ms("pool", ztp[0:1, :], 0.0, [ztp])
            else:
                T.dma("sp", ztp[:], zt_vg[col0 - 1:col0 + 127, :], w=[ztp])
            yield
            tt("dve", zs[:], zfc[:, :, 0:128], zfc[:, :, 1:129], ALU.subtract, [zfc], [zs])
            tt("dve", zs[:], zs[:], bc(mu_fm[:], [128, 13, 128], 2), ALU.mult, [zs, mu_fm], [zs])
            tt("dve", zs[:], zs[:], zfc[:, :, 1:129], ALU.add, [zs, zfc], [zs])
            yield
            if sample:
                ms("dve", zs[:, :, 1:128], 0.0, [zs])

            def child_tm():
                tt("dve", ztp[:], ztp[:], zt[:], ALU.subtract, [ztp, zt], [ztp])
                tt("dve", ztp[:], ztp[:], mu_tm[:], ALU.mult, [ztp, mu_tm], [ztp])
                yield
                tt("dve", zt[:], zt[:], ztp[:], ALU.add, [zt, ztp], [zt])
                if sample:
                    ts("dve", zt[:], zt[:], rowm0[:, 0:1], ALU.mult, [zt, rowm0], [zt])
                yield
                cp("act", S_.vbf[:], zt[:, 0:768], [zt], [S_.vbf])
                act(S_.sgb[:], zt[:, 768:1536], AF.Silu, [zt], [S_.sgb])
                yield

            def child_loc():
                act(lor[0:64, :], zs[0:64, 12, :], AF.Tanh, [zs], [lor])
                cp("dve", lor[64:128, :], zs[64:128, 12, :], [zs], [lor])
                for _ in range(LORA_DELAY):
                    yield
                bw = [pb[0], pb[1]]
                for j in range(6):
                    mm(bw[j // 4][:, (j % 4) * 128:(j % 4 + 1) * 128], lw_[0:64, j * 128:(j + 1) * 128], lor[0:64, :],
                       True, True, [lw_, lor], [bw[j // 4]])
                yield
                for j in range(6):
                    act(sigw[:, j, :], bw[j // 4][:, (j % 4) * 128:(j % 4 + 1) * 128], AF.Sigmoid, [bw[j // 4], pr], [sigw],
                        bias=pr[:, 0, j:j + 1])
                    if j % 2:
                        yield
                if sample:
                    ms("dve", sigw[:, :, 1:128], 0.0, [sigw])
                for j in range(6):
                    T.op("dve", lambda E, j=j: E.tensor_tensor_scan(out=cs[:, j, :], data0=ones[:], data1=sigw[:, j, :],
                                                                   initial=0.0, op0=ALU.mult, op1=ALU.add),
                         r=[ones, sigw], w=[cs])
                    if j % 2:
                        yield
                for j in range(6):
                    mm(bw[j // 4][:, (j % 4) * 128:(j % 4 + 1) * 128], lw_[64:128, j * 128:(j + 1) * 128], lor[64:128, :],
                       True, True, [lwa_tok, lor], [bw[j // 4]])
                yield
                for j in range(6):
                    act(aa[:, j, :], bw[j // 4][:, (j % 4) * 128:(j % 4 + 1) * 128], AF.Sigmoid, [bw[j // 4], pr], [aa],
                        bias=pr[:, 1, j:j + 1])
                    if j % 2:
                        yield
                act(tB[:], cs[:], AF.Exp, [cs], [tB], scale=-C0)
                tt("dve", tA[:], cs[:], sigw[:], ALU.subtract, [cs, sigw], [tA])
                yield
                act(tA[:], tA[:], AF.Exp, [tA], [tA], scale=-C0)
                act(enE[:], cs[:], AF.Exp, [cs], [enE], scale=C0)
                yield
                cp("dve", S_.wc[:], tB[:, :, 127], [tB], [S_.wc])
                tt("dve", S_.rhat[:], zs[:, 0:6, :], tB[:], ALU.mult, [zs, tB], [S_.rhat])
                yield

            def child_k():
                tt("dve", kk[:], zs[:, 6:12, :], bc(pr[:, 2, :], [128, 6, 128], 2), ALU.mult, [zs, pr], [kk])
                act(tK1[:], kk[:], AF.Square, [kk], [tK1])
                yield
                bq = [pb[2], pb[3]]
                tAf = tK1[:].rearrange("p a b -> p (a b)")
                mm(bq[0][:], bones[:], tAf[:, 0:512], True, True, [bones, tK1], [bq[0]])
                mm(bq[1][:, 0:256], bones[:], tAf[:, 512:768], True, True, [bones, tK1], [bq[1]])
                yield
                tBf = tK2[:].rearrange("p a b -> p (a b)")
                ts("dve", tBf[:, 0:512], bq[0][:], 1e-18, ALU.max, [bq[0]], [tK2])
                ts("dve", tBf[:, 512:768], bq[1][:, 0:256], 1e-18, ALU.max, [bq[1]], [tK2])
                yield
                act(tK2[:], tK2[:], AF.Ln, [tK2], [tK2])
                act(tK2[:], tK2[:], AF.Exp, [tK2], [tK2], scale=-0.5)
                yield
                tt("dve", kk[:], kk[:], tK2[:], ALU.mult, [kk, tK2], [kk])
                yield

            kids = [child_loc(), child_k(), child_tm()]
            while kids:
                for k_ in list(kids):
                    try:
                        next(k_)
                    except StopIteration:
                        kids.remove(k_)
                yield
            stt("dve", S_.ahat[:], kk[:], -1.0, tA[:], ALU.mult, ALU.mult, [kk, tA], [S_.ahat])
            yield
            tt("dve", tA[:], kk[:], aa[:], ALU.mult, [kk, aa], [tA])
            tt("dve", bhat[:], tA[:], enE[:], ALU.mult, [tA, enE], [bhat])
            for j in range(6):
                act(tB[:, j, :], aa[:, j, :], AF.Identity, [aa, pr, omka], [tB], scale=pr[:, 3, j:j + 1], bias=omka[:, j:j + 1])
            yield
            tt("dve", tB[:], tB[:], zs[:, 6:12, :], ALU.mult, [tB, zs], [tB])
            yield
            tt("dve", khat[:], tB[:], enE[:], ALU.mult, [tB, enE], [khat])
            tt("dve", tA[:], tB[:], zs[:, 0:6, :], ALU.mult, [tB, zs], [tA])
            tt("dve", prodb[:], tA[:], bc(pr[:, 4, :], [128, 6, 128], 2), ALU.mult, [tA, pr], [prodb])
            yield
            for _ in range(BON_DELAY):
                yield
            bk = nbp()
            for j in range(6):
                mm(bk[:, 2 * j:2 * j + 2], prodb[:, j, :], bsel[:], True, True, [prodb, bsel], [bk])
            cp("dve", S_.bon[:], bk[:, 0:12], [bk], [S_.bon])
            yield
            for (srcb, dstb) in ((bhat, S_.Bt), (khat, S_.Kt)):
                bk = nbp()
                bkb = bk[:].bitcast(BF16)
                for j in range(6):
                    tp(bkb[:, j * 128:(j + 1) * 128], srcb[:, j, :], identB[:], [srcb, identB], [bk])
                evac(dstb[:], bkb[:, 0:768].rearrange("p (a b) -> p a b", a=6), [bk], [dstb])
                yield
            specs = ((bhat, S_.ahat, m_su, S_.Nl[0]), (S_.ahat, bhat, m_sl, Ml[0]), (khat, S_.ahat, m_su, S_.MakT),
                     (bhat, S_.rhat, m_u, S_.MrbT), (khat, S_.rhat, m_u, S_.MrkT))
            for (L_, R_, mk, dst) in specs:
                for grp in ((0, 1, 2, 3), (4, 5), (6, 7, 8, 9), (10, 11)):
                    bk = nbp()
                    n_ = len(grp)
                    for idx, sl in enumerate(grp):
                        j, par = sl % 6, sl // 6
                        ps_ = slice(par * 64, par * 64 + 64)
                        mm(bk[:, idx * 128:(idx + 1) * 128], L_[ps_, j, :], R_[ps_, j, :], True, True, [L_, R_], [bk])
                    dv_ = dst[:, grp[0]:grp[0] + n_, :]
                    cp("act", dv_, bk[:, 0:n_ * 128].rearrange("p (a b) -> p a b", a=n_), [bk], U(dst, grp))
                    tt("dve", dv_, dv_, mk[:, 0:n_, :], ALU.mult, U(dst, grp) + [mk], U(dst, grp))
                    yield
            for i in range(1, 7):
                if sample:
                    break
                Np, Mp = S_.Nl[i - 1], Ml[(i - 1) % 2]
                for hg in range(3):
                    bk = nbp()
                    for hh in range(4):
                        h = hg * 4 + hh
                        mm(bk[:, hh * 128:(hh + 1) * 128], Mp[:, h, :], Np[:, h, :], True, True, U(Mp, [h]) + U(Np, [h]), [bk])
                    evac(S_.Nl[i][:, hg * 4:(hg + 1) * 4, :], bk[:].rearrange("p (a b) -> p a b", a=4), [bk],
                         U(S_.Nl[i], range(hg * 4, hg * 4 + 4)))
                    yield
                if i < 6:
                    for hg in range(3):
                        bk = nbp()
                        for hh in range(4):
                            h = hg * 4 + hh
                            mm(bk[:, hh * 128:(hh + 1) * 128], Np[:, h, :], Mp[:, h, :], True, True, U(Mp, [h]) + U(Np, [h]), [bk])
                        evac(Ml[i % 2][:, hg * 4:(hg + 1) * 4, :], bk[:].rearrange("p (a b) -> p a b", a=4), [bk],
                             U(Ml[i % 2], range(hg * 4, hg * 4 + 4)))
                        yield

        def gen_seq(gc, S_, sample=False):
            col0 = gc * 128
            ahat, rhat, vbf, Nl = S_.ahat, S_.rhat, S_.vbf, S_.Nl
            xa, xb_ = nbs(), nbs()
            for sl in range(12):
                j, par, ps_, h = sl_info(sl)
                bk = xa if par == 0 else xb_
                dst = bk[:, j * 64:(j + 1) * 64]
                mm(dst, ahat[ps_, j, :], Hbf[ps_, j, :], True, False, [ahat, Hbf], [bk])
                mm(dst, S_.MakT[:, sl, :], vbf[:, h * 64:(h + 1) * 64], False, True, U(S_.MakT, [sl]) + [vbf], [bk])
            Xc = Xb[0]
            cp("act", Xc[:, 0:384], xa[:, 0:384], [xa], [XH(Xc, 0)])
            cp("dve", Xc[:, 384:768], xb_[:, 0:384], [xb_], [XH(Xc, 1)])
            yield
            for i in range(7):
                if sample:
                    break
                xa, xb_ = nbs(), nbs()
                for sl in range(12):
                    j, par, ps_, h = sl_info(sl)
                    bk = xa if par == 0 else xb_
                    dst = bk[:, j * 64:(j + 1) * 64]
                    mm(dst, identB[:], Xc[:, sl * 64:(sl + 1) * 64], True, False, [identB, XH(Xc, par)], [bk])
                    mm(dst, Nl[i][:, sl, :], Xc[:, sl * 64:(sl + 1) * 64], False, True, U(Nl[i], [sl]) + [XH(Xc, par)], [bk])
                Xn = Xb[(i + 1) % 2]
                cp("act", Xn[:, 0:384], xa[:, 0:384], [xa], [XH(Xn, 0)])
                cp("dve", Xn[:, 384:768], xb_[:, 0:384], [xb_], [XH(Xn, 1)])
                Xc = Xn
                yield
            Uu = Xc
            oa_, ob_ = nbs(), nbs()
            for sl in range(12):
                j, par, ps_, h = sl_info(sl)
                bk = oa_ if par == 0 else ob_
                dst = bk[:, j * 64:(j + 1) * 64]
                mm(dst, rhat[ps_, j, :], Hbf[ps_, j, :], True, False, [rhat, Hbf], [bk])
                mm(dst, S_.MrbT[:, sl, :], Uu[:, sl * 64:(sl + 1) * 64], False, False, U(S_.MrbT, [sl]) + [XH(Uu, par)], [bk])
                mm(dst, S_.MrkT[:, sl, :], vbf[:, h * 64:(h + 1) * 64], False, True, U(S_.MrkT, [sl]) + [vbf], [bk])
            Oof = Oo[:].rearrange("p a b -> p (a b)")
            evac(Oo[:, bass.ds(0, 6, step=2), :], oa_[:, 0:384].rearrange("p (a b) -> p a b", a=6), [oa_], [Oo])
            evac(Oo[:, bass.ds(1, 6, step=2), :], ob_[:, 0:384].rearrange("p (a b) -> p a b", a=6), [ob_], [Oo])
            yield
            ha, hb = nbs(), nbs()
            for j in range(6):
                bk = ha if j < 4 else hb
                dst = bk[:, (j % 4) * 128:(j % 4 + 1) * 128]
                mm(dst, S_.Bt[:, j, :], Uu[:].rearrange("p (par j e) -> p par j e", par=2, j=6)[:, :, j, :], True, False,
                   [S_.Bt, XH(Uu, 0), XH(Uu, 1)], [bk])
                mm(dst, S_.Kt[:, j, :], vbf[:, j * 128:(j + 1) * 128], False, True, [S_.Kt, vbf], [bk])
            for par in range(2):
                ps_ = slice(par * 64, par * 64 + 64)
                tt("dve", Hst[ps_, 0:4, :], ha[ps_, :].rearrange("p (a b) -> p a b", a=4)[:, :, par * 64:par * 64 + 64],
                   Hst[ps_, 0:4, :], ALU.add, [ha, Hst], [Hst])
                tt("dve", Hst[ps_, 4:6, :], hb[ps_, 0:256].rearrange("p (a b) -> p a b", a=2)[:, :, par * 64:par * 64 + 64],
                   Hst[ps_, 4:6, :], ALU.add, [hb, Hst], [Hst])
                tt("dve", Hst[ps_, :, :], Hst[ps_, :, :], bc(S_.wc[ps_, :], [64, 6, 64], 2), ALU.mult, [Hst, S_.wc], [Hst])
                cp("act", Hbf[ps_, :, :], Hst[ps_, :, :], [Hst], [Hbf])
            yield
            T.op("dve", lambda E: E.tensor_reduce(out=st12[:, 0, :], in_=Oo[:], axis=AX.X, op=ALU.add), r=[Oo], w=[st12])
            act(Os[:], Oo[:], AF.Square, [Oo], [Os])
            T.op("dve", lambda E: E.tensor_reduce(out=st12[:, 1, :], in_=Os[:], axis=AX.X, op=ALU.add), r=[Os], w=[st12])
            ts("dve", st12[:, 0, :], st12[:, 0, :], 1.0 / 64, ALU.mult, [st12], [st12])
            tt("dve", st12[:, 2, :], st12[:, 0, :], st12[:, 0, :], ALU.mult, [st12], [st12])
            stt("dve", st12[:, 1, :], st12[:, 1, :], 1.0 / 64, st12[:, 2, :], ALU.mult, ALU.subtract, [st12], [st12])
            act(st12[:, 3, :], st12[:, 1, :], AF.Sqrt, [st12, eps_g], [st12], bias=eps_g[:, 0:1])
            T.op("dve", lambda E: E.reciprocal(out=st12[:, 3, :], in_=st12[:, 3, :]), r=[st12], w=[st12])
            yield
            tt("dve", Os[:], Oo[:], bc(st12[:, 0, :], [128, 12, 64], 2), ALU.subtract, [Oo, st12], [Os])
            tt("dve", Os[:], Os[:], bc(st12[:, 3, :], [128, 12, 64], 2), ALU.mult, [Os, st12], [Os])
            Osf = Os[:].rearrange("p a b -> p (a b)")
            tt("dve", Osf, Osf, lnw[:], ALU.mult, [Os, lnw], [Os])
            tt("dve", Osf, Osf, lnb[:], ALU.add, [Os, lnb], [Os])
            yield
            tt("dve", Oo[:], vbf[:].rearrange("p (a b) -> p a b", a=12), bc(S_.bon[:], [128, 12, 64], 2), ALU.mult,
               [vbf, S_.bon], [Oo])
            tt("dve", Os[:], Os[:], Oo[:], ALU.add, [Os, Oo], [Os])
            tt("dve", mixa[:], Osf, S_.sgb[:], ALU.mult, [Os, S_.sgb], [mixa])
            for _ in range(MIX_DELAY):
                yield
            bk = nbs()
            bkb = bk[:].bitcast(BF16)
            for j in range(6):
                tp(bkb[:, j * 128:(j + 1) * 128], mixa[:, j * 128:(j + 1) * 128], identB[:], [mixa, identB], [bk])
            evac(big[:, 0:6, col0:col0 + 128], bkb[:, 0:768].rearrange("p (a b) -> p a b", a=6), [bk], [btok[gc]])
            yield

        def run_gens(gens):
            active = list(gens)
            while active:
                for it_ in list(active):
                    g_, w_ = it_
                    for _ in range(w_):
                        try:
                            next(g_)
                        except StopIteration:
                            active.remove(it_)
                            break

        def rwkv_seq(chunks, sample):
            if not sample:
                ms("dve", Hst[:], 0.0, [Hst])
                ms("dve", Hbf[:], 0.0, [Hbf])
            else:
                sv = I["st_wkv"][l].rearrange("(jp jj par) v k -> jj v jp par k", jp=3, jj=2, par=2)
                for jj in range(2):
                    for jp in range(3):
                        T.dma("sp", hout[jj * 64:(jj + 1) * 64, jp, :].rearrange("p (par k) -> p par k", par=2), sv[jj, :, jp], w=[hout])
                bk = nb()
                for jp in range(3):
                    tp(bk[:, jp * 128:(jp + 1) * 128], hout[:, jp, :], identF[:], [hout, identF], [bk])
                cp("dve", Hst[:].rearrange("p a b -> p (a b)"), bk[:, 0:384], [bk], [Hst])
                cp("act", Hbf[:], Hst[:], [Hst], [Hbf])
            n_ = len(chunks)
            for ci in range(n_ + 1):
                gens = []
                if ci > 0:
                    gens.append((gen_seq(chunks[ci - 1], sets[(ci - 1) % 2], sample), 1))
                if ci < n_:
                    gens.append((gen_prep(chunks[ci], ci == 0, sample, sets[ci % 2]), PREP_W))
                run_gens(gens)
            dv = O["s_wkv" if sample else "p_wkv"][l].rearrange("(jp jj par) v k -> jj v jp par k", jp=3, jj=2, par=2)
            bk = nb()
            Hf = Hst[:].rearrange("p a b -> p (a b)")
            for jp in range(3):
                tp(bk[:, jp * 128:(jp + 1) * 128], Hf[:, jp * 128:(jp + 1) * 128], identF[:], [Hst, identF], [bk])
            evac(hout[:].rearrange("p a b -> p (a b)"), bk[:, 0:384], [bk], [hout])
            for jj in range(2):
                for jp in range(3):
                    T.dma("sp", dv[jj, :, jp], hout[jj * 64:(jj + 1) * 64, jp, :].rearrange("p (par k) -> p par k", par=2), r=[hout])

        if "p2" in phases:
            rwkv_seq(PCH, False)
            rwkv_seq([16], True)
        T.pop()

        T.push()
        qT = T.sb([128, 6, NT], BF16, "qT")
        kT = T.sb([128, 6, NT], BF16, "kT")
        nw = T.sb([128, 2], F32, "nw")
        cosT = T.sb([128, NT], F32, "cosT")
        sinT = T.sb([128, NT], F32, "sinT")
        T.dma("sp", cosT[:], ropetab[0], w=[cosT])
        T.dma("sp", sinT[:], ropetab[1], w=[sinT])
        for hb_ in range(2):
            T.dma("sp", nw[hb_ * 64:(hb_ + 1) * 64, 0:1], I["q_norm_w"][l].rearrange("(p o) -> p o", o=1), w=[nw])
            T.dma("sp", nw[hb_ * 64:(hb_ + 1) * 64, 1:2], I["k_norm_w"][l].rearrange("(p o) -> p o", o=1), w=[nw])
        ts("dve", nw[:, 0:1], nw[:, 0:1], 0.125, ALU.mult, [nw], [nw])

        pipe_depth = [2]

        def slot_banks(slot):
            st = [0]
            nper = 8 // pipe_depth[0]

            def f():
                st[0] = (st[0] + 1) % nper
                return pb[slot * nper + st[0]]
            return f

        def run_pipe(factories, depth=2):
            pipe_depth[0] = depth
            pending = list(factories)
            active = {}
            while pending or active:
                for sl_ in range(depth):
                    if sl_ not in active and pending:
                        active[sl_] = pending.pop(0)(sl_)
                for sl_ in list(active.keys()):
                    try:
                        next(active[sl_])
                    except StopIteration:
                        del active[sl_]

        if "p3" in phases:
            T.push()
            QD = 4
            zq = [T.sb([128, 512], F32, "zq") for _ in range(QD)]
            sq = [T.sb([128, 512], F32, "sq") for _ in range(QD)]
            rs = [T.sb([128, 512], F32, "rs") for _ in range(QD)]
            zn = [T.sb([128, 512], F32, "zn") for _ in range(QD)]
            kfin = [T.sb([128, 512], F32, "kfin") for _ in range(QD)]
            kout = [T.sb([128, 128], F32, "kout") for _ in range(QD)]

            def qk_block(ti, tb):
                def g_(slot):
                    nbk = slot_banks(slot)
                    isk = ti >= 6
                    n = 512 if tb < 4 else 128
                    cs_ = slice(tb * 512, tb * 512 + n)
                    z_, sq_, rs_, zn_, kf_, ko = zq[slot], sq[slot], rs[slot], zn[slot], kfin[slot], kout[slot]
                    T.dma("sp", z_[:, 0:n], zf[(13 + ti) * 128:(14 + ti) * 128, cs_], w=[z_])
                    yield
                    tt("pool", sq_[:, 0:n], z_[:, 0:n], z_[:, 0:n], ALU.mult, [z_], [sq_])
                    bk = nbk()
                    mm(bk[:, 0:n], bones[:], sq_[:, 0:n], True, True, [bones, sq_], [bk])
                    yield
                    act(rs_[:, 0:n], bk[:, 0:n], AF.Ln, [bk, eps_n], [rs_], scale=1.0 / 64, bias=eps_n[:, 0:1])
                    act(rs_[:, 0:n], rs_[:, 0:n], AF.Exp, [rs_], [rs_], scale=-0.5)
                    yield
                    stt("dve", zn_[:, 0:n], z_[:, 0:n], nw[:, (1 if isk else 0):(2 if isk else 1)], rs_[:, 0:n], ALU.mult,
                        ALU.mult, [z_, nw, rs_], [zn_])
                    bk2 = nbk()
                    mm(bk2[:, 0:n], prot[:], zn_[:, 0:n], True, True, [prot, zn_], [bk2])
                    yield
                    tt("dve", sq_[:, 0:n], zn_[:, 0:n], cosT[:, cs_], ALU.mult, [zn_, cosT], [sq_])
                    tt("dve", rs_[:, 0:n], bk2[:, 0:n], sinT[:, cs_], ALU.mult, [bk2, sinT], [rs_])
                    yield
                    if not isk:
                        tt("dve", qT[:, ti, cs_], sq_[:, 0:n], rs_[:, 0:n], ALU.add, [sq_, rs_], [qT])
                        yield
                    else:
                        tt("dve", kf_[:, 0:n], sq_[:, 0:n], rs_[:, 0:n], ALU.add, [sq_, rs_], [kf_])
                        cp("act", kT[:, ti - 6, cs_], kf_[:, 0:n], [kf_], [kT])
                        yield
                        g, half = (ti - 6) // 2, (ti - 6) % 2
                        for cc in range(n // 128):
                            gc = tb * 4 + cc
                            need = (gc == 16) or (g == 0 and gc == 15) or (g == 1 and gc >= 12) or g == 2
                            if not need:
                                continue
                            bk3 = nbk()
                            tp(bk3[:, 0:128], kf_[:, cc * 128:(cc + 1) * 128], identF[:], [kf_, identF], [bk3])
                            evac(ko[:], bk3[:, 0:128], [bk3], [ko])
                            hc = slice(half * 128, half * 128 + 128)
                            if gc == 16:
                                T.dma("sp", O["s_k%d" % g][l, 0:1, hc], ko[0:1, :], r=[ko])
                            else:
                                keep = (128, 512, 2048)[g]
                                r0 = gc * 128 - (2048 - keep)
                                T.dma("sp", O["p_k%d" % g][l, r0:r0 + 128, hc], ko[:], r=[ko])
                            yield
                return g_

            run_pipe([qk_block(ti, tb) for ti in range(12) for tb in range(5)], depth=QD)
            T.pop()
            T.push()
            kTc = T.sb([128, 3, 2, 128], BF16, "kTc")
            Va = T.sb([128, 3, 17, 4, 65], BF16, "Va")
            Vc = T.sb([128, 3, 4, 65], BF16, "Vc")
            ms("pool", Va[:], 1.0, [Va])
            ms("pool", Vc[:], 1.0, [Vc])
            vst = [T.sb([128, 256], F32, "vst") for _ in range(6)]
            it = 0
            for g in range(3):
                for bi in range(17):
                    v_ = vst[it % 6]
                    it += 1
                    T.dma("sp", v_[:], zt_av[g, bi * 128:(bi + 1) * 128, :], w=[v_])
                    cp("act" if it % 2 else "dve", Va[:, g, bi, :, 0:64], v_[:].rearrange("p (a b) -> p a b", a=4), [v_], [Va])
                d = DILS[g]
                v_ = vst[it % 6]
                it += 1
                T.dma("sp", v_[:], I["cv%d" % g][l].rearrange("(r d) c -> r d c", d=d)[:, 0, :], w=[v_])
                cp("dve", Vc[:, g, :, 0:64], v_[:].rearrange("p (a b) -> p a b", a=4), [v_], [Vc])
                v_ = vst[it % 6]
                it += 1
                T.dma("sp", v_[:], I["ck%d" % g][l].rearrange("(r d) c -> r d c", d=d)[:, 0, :], w=[v_])
                for half in range(2):
                    bk = nb()
                    tp(bk[:, 0:128], v_[:, half * 128:(half + 1) * 128], identF[:], [v_, identF], [bk])
                    evac(kTc[:, g, half, :], bk[:, 0:128], [bk], [kTc])
            pT = [T.sb([128, 2, 2, 2, 128], BF16, "pT") for _ in range(2)]
            ost = [T.sb([128, 260], F32, "ost") for _ in range(2)]
            csl = (lambda s0, st_: slice(s0, s0 + 128) if st_ == 1 else bass.ds(s0, 128, step=st_))

            def att_block(g, bi):
                def g_(slot):
                    nbk = slot_banks(slot)
                    start, step = blk_cols(g, bi)
                    prev = blk_prev(g, bi)
                    cur = csl(start, step)
                    p_ = pT[slot]
                    o_ = ost[slot]
                    for h2 in range(2):
                        bk = nbk()
                        bkv = bk[:].rearrange("p (a b c) -> p a b c", a=2, b=2)
                        ps_ = slice(h2 * 64, h2 * 64 + 64)
                        for half in range(2):
                            j = g * 2 + half
                            if prev is not None:
                                if prev == "cache":
                                    lhs = kTc[ps_, g, half, :]
                                else:
                                    ps0, pst = blk_cols(g, prev)
                                    lhs = kT[ps_, j, csl(ps0, pst)]
                                mm(bkv[:, half, 0, :], lhs, qT[ps_, j, cur], True, True, [kT, kTc, qT], [bk])
                            mm(bkv[:, half, 1, :], kT[ps_, j, cur], qT[ps_, j, cur], True, True, [kT, qT], [bk])
                        yield
                        if prev is not None:
                            act(p_[:, :, h2], bkv, AF.Exp, [bk], [p_])
                            tt("dve", p_[:, :, h2], p_[:, :, h2], amask[:], ALU.mult, [p_, amask], [p_])
                        else:
                            act(p_[:, :, h2, 1, :], bkv[:, :, 1, :], AF.Exp, [bk], [p_])
                            tt("dve", p_[:, :, h2, 1, :], p_[:, :, h2, 1, :], amask[:, :, 1, :], ALU.mult, [p_, amask], [p_])
                        yield
                    bo = nbk()
                    for hl in range(4):
                        half, h2 = hl // 2, hl % 2
                        dst = bo[:, hl * 65:(hl + 1) * 65]
                        if prev is not None:
                            rv = Vc[:, g, hl, :] if prev == "cache" else Va[:, g, prev, hl, :]
                            mm(dst, p_[:, half, h2, 0, :], rv, True, False, [p_, Va, Vc], [bo])
                        mm(dst, p_[:, half, h2, 1, :], Va[:, g, bi, hl, :], prev is None, True, [p_, Va], [bo])
                    yield
                    evac(o_[:], bo[:, 0:260], [bo], [o_])
                    if bi == 16:
                        dst = oacc[g, 2048:NT, :]
                    else:
                        d = DILS[g]
                        i0 = (start - start % d) // d
                        dst = oacc[g, 0:2048, :].rearrange("(i d) c -> i d c", d=d)[i0:i0 + 128, start % d, :]
                    T.dma("sp", dst, o_[:], r=[o_])
                    yield
                return g_

            run_pipe([att_block(g, bi) for g in range(3) for bi in range(17)])
            T.pop()
            T.pop()
            T.push()
            oa3 = [T.sb([128, 3, 260], F32, "oa3") for _ in range(2)]
            zg = [T.sb([128, 768], F32, "zg") for _ in range(2)]
            ls = [T.sb([128, 4], F32, "ls") for _ in range(2)]
            mixb = [T.sb([128, 3, 4, 64], F32, "mixb") for _ in range(2)]
            mixbb = [T.sb([128, 768], BF16, "mixbb") for _ in range(2)]

            def comb_block(gc):
                def g_(slot):
                    nbk = slot_banks(slot)
                    o3, gz, ls_, mb, mbb = oa3[slot], zg[slot], ls[slot], mixb[slot], mixbb[slot]
                    T.dma("sp", o3[:], oacc[:, gc * 128:(gc + 1) * 128, :].rearrange("g t c -> t g c"), w=[o3])
                    T.dma("sp", gz[:], zt_ag[gc * 128:(gc + 1) * 128, :], w=[gz])
                    yield
                    o3v = o3[:].rearrange("p g (h e) -> p g h e", h=4)
                    tt("dve", ls_[:], o3v[:, 0, :, 64], o3v[:, 1, :, 64], ALU.add, [o3], [ls_])
                    tt("dve", ls_[:], ls_[:], o3v[:, 2, :, 64], ALU.add, [ls_, o3], [ls_])
                    T.op("dve", lambda E: E.reciprocal(out=ls_[:], in_=ls_[:]), r=[ls_], w=[ls_])
                    act(gz[:], gz[:], AF.Silu, [gz], [gz])
                    yield
                    for g in range(3):
                        tt("dve", mb[:, g], o3v[:, g, :, 0:64], bc(ls_[:], [128, 4, 64], 2), ALU.mult, [o3, ls_], [mb])
                    yield
                    tt("dve", mbb[:], mb[:].rearrange("p g h e -> p (g h e)"), gz[:], ALU.mult, [mb, gz], [mbb])
                    bk = nbk()
                    bkb = bk[:].bitcast(BF16)
                    for j in range(6):
                        tp(bkb[:, j * 128:(j + 1) * 128], mbb[:, j * 128:(j + 1) * 128], identB[:], [mbb, identB], [bk])
                    yield
                    evac(big[:, 6:12, gc * 128:(gc + 1) * 128], bkb[:, 0:768].rearrange("p (a b) -> p a b", a=6), [bk], [btok[gc]])
                    yield
                return g_

            run_pipe([comb_block(gc) for gc in range(NCH)])
        T.pop()

        T.push()
        wo = [T.sb([128, 16, 512], BF16, "wo") for _ in range(4)]
        if "p5" in phases:
            for cb in range(4):
                T.dma("pool", wo[cb][:], I["w_out"][l][:, cb * 512:(cb + 1) * 512].rearrange("(k p) c -> p k c", p=128), w=[wo[cb]])
        T.push()
        if "p4" in phases:
            pw = T.sb([128, 4, 128], BF16, "pw")
            T.dma("pool", pw[:], I["pool_w"][l].rearrange("g c d -> c g d"), w=[pw])
            psc = T.sb([128, 4], F32, "psc")
            T.dma("sp", psc[:], I["pool_scale"][l].rearrange("(g p) -> p g", p=128), w=[psc])
            WB = 16 + 2048
            ub = T.sb([128, WB], F32, "ub")
            sA = T.sb([128, WB], F32, "sA")
            sB = T.sb([128, WB], F32, "sB")
            db = T.sb([128, 2048], BF16, "db")
            gt = T.sb([128, 2048], F32, "gt")
            for gi in range(4):
                wsz = 2 ** (gi + 1)
                for sample in (False, True):
                    Wn = WB if not sample else 17
                    nreal = 2048 if not sample else 1
                    ccol = 0 if not sample else 2048
                    T.dma("sp", ub[:, 16:16 + nreal], zf[(25 + gi) * 128:(26 + gi) * 128, ccol:ccol + nreal], w=[ub])
                    ms("pool", ub[:, 0:16], 0.0, [ub])
                    if sample:
                        T.dma("sp", ub[:, 1:16], I["st_pool"][l, :, gi * 128:(gi + 1) * 128].rearrange("r c -> c r"), w=[ub])
                    T.dma("sp", gt[:, 0:nreal], zf[(29 + gi) * 128:(30 + gi) * 128, ccol:ccol + nreal], w=[gt])
                    s_c = ub
                    for k_ in range(gi + 1):
                        stp = 2 ** k_
                        s_n = sA if k_ % 2 == 0 else sB
                        tt("dve" if k_ % 2 == 0 else "pool", s_n[:, stp:Wn], s_c[:, stp:Wn], s_c[:, 0:Wn - stp], ALU.add,
                           [s_c], [s_n])
                        cp("pool", s_n[:, 0:stp], s_c[:, 0:stp], [s_c], [s_n])
                        s_c = s_n
                    s_o = sA if s_c is sB else sB
                    ts("dve", s_o[:, 16:Wn], s_c[:, 16:Wn], 1.0 / wsz, ALU.mult, [s_c], [s_o])
                    if not sample:
                        tt("dve", s_o[:, 16:32], s_o[:, 16:32], fixw[:, gi, :], ALU.mult, [s_o, fixw], [s_o])
                    tt("dve", db[:, 0:nreal], s_o[:, 16:Wn], ub[:, 16:Wn], ALU.subtract, [s_o, ub], [db])
                    act(gt[:, 0:nreal], gt[:, 0:nreal], AF.Silu, [gt], [gt])
                    ts("dve", gt[:, 0:nreal], gt[:, 0:nreal], psc[:, gi:gi + 1], ALU.mult, [gt, psc], [gt])
                    nblk = 4 if not sample else 1
                    for tb in range(nblk):
                        n = 512 if not sample else 1
                        bk = nb()
                        mm(bk[:, 0:n], pw[:, gi, :], db[:, tb * 512:tb * 512 + n], True, True, [pw, db], [bk])
                        toks = btok[tb * 4:tb * 4 + 4] if not sample else [btok[16]]
                        tt("dve", big[:, 12 + gi, ccol + tb * 512:ccol + tb * 512 + n], bk[:, 0:n], gt[:, tb * 512:tb * 512 + n],
                           ALU.mult, [bk, gt], toks)
            ms("pool", big[:, 12:16, 2049:NT], 0.0, [btok[16]])
        T.pop()

        if "p5" in phases:
            fuse = FUSE_NORM and l < DEPTH - 1
            xy = [T.sb([128, D], F32, "xy") for _ in range(3)]
            if fuse:
                normw2 = T.sb([128, 16], F32, "normw2")
                T.dma("sp", normw2[:], I["norm_w"][l + 1].rearrange("(k p) -> p k", p=128), w=[normw2])
                junk2 = T.sb([128, D], BF16, "junk2")
                ssq2 = T.sb([128, 2], F32, "ssq2")
            for gc in range(NCH):
                x_ = xy[gc % 3]
                src, rows = x_src(l, gc)
                rd = [xtok[(l - 1) % 2][gc]] if l > 0 else []
                if rows == 1:
                    ms("pool", x_[:], 0.0, [x_])
                T.dma("sp", x_[0:rows, :], src, r=rd, w=[x_])
                for cb in range(4):
                    w_ = wo[cb]
                    bk = nb()
                    for kt in range(16):
                        mm(bk[:], big[:, kt, gc * 128:(gc + 1) * 128], w_[:, kt, :], kt == 0, kt == 15, [w_, btok[gc]], [bk])
                    tt("dve", x_[0:rows, cb * 512:(cb + 1) * 512], bk[0:rows, :], x_[0:rows, cb * 512:(cb + 1) * 512], ALU.add,
                       [bk, x_], [x_])
                dst, rows = y_dst(l, gc)
                T.dma("sp", dst, x_[0:rows, :], r=[x_], w=[xtok[l % 2][gc]])
                if fuse and gc > 0:
                    norm_chunk(xy[(gc - 1) % 3], gc - 1, normw2, junk2, ssq2)
            if fuse:
                norm_chunk(xy[(NCH - 1) % 3], NCH - 1, normw2, junk2, ssq2)
        T.pop()

    T.finish()
    cm2.__exit__(None, None, None)
    cm.__exit__(None, None, None)
    return nc


_NC_CACHE = {}


def make_in_maps(inputs):
    f = lambda a: np.ascontiguousarray(np.asarray(a, dtype=np.float32))
    g = {k: f(v) for k, v in inputs.items()}
    maps = []
    for c in range(8):
        b = c % 4
        m = {
            "x_p": g["x_prompt"][b], "x_s": g["x_sample"][c],
            "st_wkv": f(g["state_wkv"][:, c]), "st_shift": f(g["state_shift"][:, c]),
            "st_pool": f(g["state_pool"][:, c]),
            "ck0": f(g["cache_k_w128"][:, c].reshape(4, 128, 256)), "cv0": f(g["cache_v_w128"][:, c].reshape(4, 128, 256)),
            "ck1": f(g["cache_k_w512"][:, c].reshape(4, 512, 256)), "cv1": f(g["cache_v_w512"][:, c].reshape(4, 512, 256)),
            "ck2": f(g["cache_k_w2048"][:, c].reshape(4, 2048, 256)), "cv2": f(g["cache_v_w2048"][:, c].reshape(4, 2048, 256)),
            "norm_w": g["norm_w"], "w_in": g["w_in"], "w_out": g["w_out"], "rwkv_mu": g["rwkv_mu"],
            "rwkv_w0": g["rwkv_w0"], "rwkv_w_up": g["rwkv_w_up"], "rwkv_a0": g["rwkv_a0"], "rwkv_a_up": g["rwkv_a_up"],
            "rwkv_k_k": g["rwkv_k_k"], "rwkv_k_a": g["rwkv_k_a"], "rwkv_r_k": f(g["rwkv_r_k"].reshape(4, 768)),
            "rwkv_ln_w": g["rwkv_ln_w"], "rwkv_ln_b": g["rwkv_ln_b"], "q_norm_w": g["q_norm_w"],
            "k_norm_w": g["k_norm_w"], "pool_w": g["pool_w"], "pool_scale": g["pool_scale"],
        }
        maps.append(m)
    return maps


def assemble(res):
    R = res
    st = lambda n, cores: np.stack([R[c][n] for c in cores], axis=1)
    pc = [0, 1, 2, 3]
    sc = list(range(8))
    out = [
        np.stack([R[c]["y_p"] for c in pc], axis=0),
        np.stack([R[c]["y_s"] for c in sc], axis=0),
        st("p_wkv", pc), st("p_shift", pc), st("p_pool", pc),
    ]
    for g, keep in enumerate((128, 512, 2048)):
        out.append(st("p_k%d" % g, pc).reshape(4, 4, keep, 4, 64))
        out.append(st("p_v%d" % g, pc).reshape(4, 4, keep, 4, 64))
    out += [st("s_wkv", sc), st("s_shift", sc), st("s_pool", sc)]
    for g in range(3):
        out.append(st("s_k%d" % g, sc).reshape(4, 8, 1, 4, 64))
        out.append(st("s_v%d" % g, sc).reshape(4, 8, 1, 4, 64))
    return tuple(np.ascontiguousarray(o, dtype=np.float32) for o in out)


def kernel(**inputs):
    nc = build()
    maps = make_in_maps(inputs)
    res = run_bass_kernel_spmd(nc, maps, core_ids=list(range(8)))
    return assemble(res.results)
```
